# Optimizing a Trainium2 kernel written in Bass

```python
import jax, jax.numpy as jnp
from jax import lax
import numpy as np

D_MODEL = 2048
BATCH = 1
SEQ = 8192
DEPTH = 4

HEAD_DIM = 128
N_RET_HEADS = 6
N_ATT_HEADS = 6
N_GMLP_GROUPS = 4
RET_WIDTH = N_RET_HEADS * HEAD_DIM
ATT_WIDTH = N_ATT_HEADS * HEAD_DIM
GMLP_WIDTH = N_GMLP_GROUPS * HEAD_DIM
MIX_WIDTH = RET_WIDTH + ATT_WIDTH + GMLP_WIDTH
IN_SPLITS = (RET_WIDTH, RET_WIDTH, RET_WIDTH, RET_WIDTH,
             ATT_WIDTH, ATT_WIDTH, ATT_WIDTH,
             GMLP_WIDTH, GMLP_WIDTH)
IN_WIDTH = sum(IN_SPLITS)
RET_CHUNK = 128
ATT_BLOCK = 128
GMLP_CHUNK = 128
DILATED_PATTERNS = ((128, 1), (512, 4), (2048, 16))
D_FF = 5632
CONV_WIDTH = 3
ROPE_BASE = 10000.0
EPS = 1e-6
NEG_INF = -1e30

kernel_name = "hybrid_retention_dilated_gmlp_trunk"


def rms_norm(x, w):
    xf = x.astype(jnp.float32)
    y = xf * lax.rsqrt(jnp.mean(xf * xf, axis=-1, keepdims=True) + EPS)
    return (y * w.astype(jnp.float32)).astype(x.dtype)


def layer_norm(x, w):
    xf = x.astype(jnp.float32)
    mu = jnp.mean(xf, axis=-1, keepdims=True)
    xc = xf - mu
    y = xc * lax.rsqrt(jnp.mean(xc * xc, axis=-1, keepdims=True) + EPS)
    return (y * w.astype(jnp.float32)).astype(x.dtype)


def rotary(x, pos):
    half = x.shape[-1] // 2
    inv_freq = ROPE_BASE ** (-jnp.arange(half, dtype=jnp.float32) / half)
    ang = pos.astype(jnp.float32)[:, None] * inv_freq[None, :]
    cos = jnp.cos(ang)[None, :, None, :]
    sin = jnp.sin(ang)[None, :, None, :]
    x1 = x[..., :half].astype(jnp.float32)
    x2 = x[..., half:].astype(jnp.float32)
    return jnp.concatenate([x1 * cos - x2 * sin, x1 * sin + x2 * cos], axis=-1)


def retention(q, k, v, pos):
    B, S, H, Dh = q.shape
    C = RET_CHUNK
    N = S // C
    q = rotary(q, pos)
    k = rotary(k, pos) * (Dh ** -0.5)
    v = v.astype(jnp.float32)
    log_g = jnp.log1p(-jnp.exp2(-5.0 - jnp.arange(H, dtype=jnp.float32)))
    idx = jnp.arange(C, dtype=jnp.float32)
    diff = idx[:, None] - idx[None, :]
    decay_in = jnp.where(diff[None] >= 0,
                         jnp.exp(jnp.maximum(diff, 0.0)[None] * log_g[:, None, None]), 0.0)
    xi = jnp.exp((idx[:, None] + 1.0) * log_g[None, :])
    zeta = jnp.exp((C - 1.0 - idx)[:, None] * log_g[None, :])
    chunk_decay = jnp.exp(C * log_g)
    qc = q.reshape(B, N, C, H, Dh)
    kc = k.reshape(B, N, C, H, Dh)
    vc = v.reshape(B, N, C, H, Dh)
    scores = jnp.einsum('bnqhd,bnkhd->bnhqk', qc, kc) * decay_in
    inner = jnp.einsum('bnhqk,bnkhe->bnqhe', scores, vc)
    upd = jnp.einsum('bnkhd,bnkhe->nbhde', kc * zeta[:, :, None], vc)

    def step(state, u):
        return chunk_decay[None, :, None, None] * state + u, state

    _, state_prev = lax.scan(step, jnp.zeros((B, H, Dh, Dh), jnp.float32), upd)
    cross = jnp.einsum('bnqhd,nbhde->bnqhe', qc * xi[:, :, None], state_prev)
    return (inner + cross).reshape(B, S, H, Dh)


def dilated_branch(q, k, v, window, dilation):
    B, S, H, Dh = q.shape
    blk = ATT_BLOCK
    steps = window // dilation
    span = dilation * blk
    s_pad = -(-S // span) * span
    L = s_pad // dilation
    nb = L // blk

    def to_blocks(t):
        t = jnp.pad(t.astype(jnp.float32), ((0, 0), (0, s_pad - S), (0, 0), (0, 0)))
        t = t.reshape(B, L, dilation, H, Dh).transpose(0, 3, 2, 1, 4)
        return t.reshape(B, H, dilation, nb, blk, Dh)

    def with_prev(t):
        prev = jnp.pad(t, ((0, 0), (0, 0), (0, 0), (1, 0), (0, 0), (0, 0)))[:, :, :, :-1]
        return jnp.concatenate([prev, t], axis=4)

    qb = to_blocks(q)
    kk = with_prev(to_blocks(k))
    vv = with_prev(to_blocks(v))
    scores = jnp.einsum('bhrnqd,bhrnkd->bhrnqk', qb, kk) * (Dh ** -0.5)
    a = jnp.arange(blk)[:, None]
    c = jnp.arange(2 * blk)[None, :]
    dist = a + blk - c
    band = (dist >= 0) & (dist <= steps)
    mask = band[None] & ((jnp.arange(nb)[:, None, None] > 0) | (c >= blk)[None])
    scores = jnp.where(mask, scores, NEG_INF)
    m = jnp.max(scores, axis=-1, keepdims=True)
    p = jnp.exp(scores - m)
    den = jnp.sum(p, axis=-1, keepdims=True)
    out = jnp.einsum('bhrnqk,bhrnkd->bhrnqd', p, vv) / den
    lse = m[..., 0] + jnp.log(den[..., 0])
    out = out.reshape(B, H, dilation, L, Dh).transpose(0, 3, 2, 1, 4).reshape(B, s_pad, H, Dh)[:, :S]
    lse = lse.reshape(B, H, dilation, L).transpose(0, 3, 2, 1).reshape(B, s_pad, H)[:, :S]
    return out, lse


def dilated_attention(q, k, v):
    outs, lses = [], []
    for window, dilation in DILATED_PATTERNS:
        o, l = dilated_branch(q, k, v, window, dilation)
        outs.append(o)
        lses.append(l)
    wts = jax.nn.softmax(jnp.stack(lses, axis=0), axis=0)
    return jnp.einsum('pbsh,pbshd->bshd', wts, jnp.stack(outs, axis=0))


def chunk_gmlp(u, v, ln_w, ws, bs):
    B, S, _ = u.shape
    C = GMLP_CHUNK
    G = N_GMLP_GROUPS
    N = S // C
    u = jax.nn.gelu(u)
    v = layer_norm(jax.nn.gelu(v), ln_w)
    tri = jnp.tril(jnp.ones((C, C), dtype=bool))
    wm = jnp.where(tri[None], ws, jnp.zeros_like(ws))
    vc = v.reshape(B, N, C, G, HEAD_DIM)
    sp = jnp.einsum('gqk,bnkgc->bnqgc', wm, vc) + bs.T[None, None, :, :, None]
    return u * sp.reshape(B, S, GMLP_WIDTH)


def conv_ffn(x, w_up, conv_w, conv_b, w_down):
    S = x.shape[1]
    h = x @ w_up
    hp = jnp.pad(h, ((0, 0), (CONV_WIDTH - 1, 0), (0, 0)))
    acc = conv_b + conv_w[0] * hp[:, 0:S]
    for j in range(1, CONV_WIDTH):
        acc = acc + conv_w[j] * hp[:, j:j + S]
    gate, val = jnp.split(acc, 2, axis=-1)
    return (jax.nn.silu(gate) * val) @ w_down


def hybrid_layer(x, pos, norm1_w, w_in, ret_norm_w, att_norm_w, gmlp_ln_w, gmlp_ws, gmlp_bs,
                 gmlp_out_w, w_out, norm2_w, w_up, conv_w, conv_b, w_down):
    B, S, _ = x.shape
    h = rms_norm(x, norm1_w)
    proj = h @ w_in
    offsets = [int(o) for o in np.cumsum(IN_SPLITS)[:-1]]
    rq, rk, rv, rg, aq, ak, av, gu, gv = jnp.split(proj, offsets, axis=-1)

    def heads(t):
        return t.reshape(B, S, -1, HEAD_DIM)

    ret = retention(heads(rq), heads(rk), heads(rv), pos)
    ret = rms_norm(ret, ret_norm_w.reshape(N_RET_HEADS, HEAD_DIM)).reshape(B, S, RET_WIDTH)
    ret = jax.nn.silu(rg.astype(jnp.float32)) * ret
    att = dilated_attention(heads(aq), heads(ak), heads(av)).reshape(B, S, ATT_WIDTH)
    att = rms_norm(att, att_norm_w)
    gm = rms_norm(chunk_gmlp(gu, gv, gmlp_ln_w, gmlp_ws, gmlp_bs), gmlp_out_w)

    mixed = jnp.concatenate([ret.astype(x.dtype), att.astype(x.dtype), gm.astype(x.dtype)], axis=-1)
    x = x + mixed @ w_out
    x = x + conv_ffn(rms_norm(x, norm2_w), w_up, conv_w, conv_b, w_down)
    return x


def setup_inputs(seed: int = 0) -> dict:
    key = jax.random.key(seed)
    ks = jax.random.split(key, 16)
    f32 = jnp.float32

    def gain(k, shape):
        return 1.0 + 0.02 * jax.random.normal(k, shape, f32)

    return {
        "x": jax.random.normal(ks[0], (BATCH, SEQ, D_MODEL), f32),
        "norm1_w": gain(ks[1], (DEPTH, D_MODEL)),
        "w_in": jax.random.normal(ks[2], (DEPTH, D_MODEL, IN_WIDTH), f32) * D_MODEL ** -0.5,
        "ret_norm_w": gain(ks[3], (DEPTH, RET_WIDTH)),
        "att_norm_w": gain(ks[4], (DEPTH, ATT_WIDTH)),
        "gmlp_ln_w": gain(ks[5], (DEPTH, GMLP_WIDTH)),
        "gmlp_ws": jax.random.normal(ks[6], (DEPTH, N_GMLP_GROUPS, GMLP_CHUNK, GMLP_CHUNK), f32) * GMLP_CHUNK ** -0.5,
        "gmlp_bs": 1.0 + 0.01 * jax.random.normal(ks[7], (DEPTH, N_GMLP_GROUPS, GMLP_CHUNK), f32),
        "gmlp_out_w": gain(ks[8], (DEPTH, GMLP_WIDTH)),
        "w_out": jax.random.normal(ks[9], (DEPTH, MIX_WIDTH, D_MODEL), f32) * MIX_WIDTH ** -0.5,
        "norm2_w": gain(ks[10], (DEPTH, D_MODEL)),
        "w_up": jax.random.normal(ks[11], (DEPTH, D_MODEL, 2 * D_FF), f32) * D_MODEL ** -0.5,
        "conv_w": jax.random.normal(ks[12], (DEPTH, CONV_WIDTH, 2 * D_FF), f32) * CONV_WIDTH ** -0.5,
        "conv_b": 0.01 * jax.random.normal(ks[13], (DEPTH, 2 * D_FF), f32),
        "w_down": jax.random.normal(ks[14], (DEPTH, D_FF, D_MODEL), f32) * D_FF ** -0.5,
        "final_norm_w": gain(ks[15], (D_MODEL,)),
    }


def reference(x, norm1_w, w_in, ret_norm_w, att_norm_w, gmlp_ln_w, gmlp_ws, gmlp_bs, gmlp_out_w,
              w_out, norm2_w, w_up, conv_w, conv_b, w_down, final_norm_w):
    pos = jnp.arange(x.shape[1], dtype=jnp.int32)
    for l in range(DEPTH):
        x = hybrid_layer(x, pos, norm1_w[l], w_in[l], ret_norm_w[l], att_norm_w[l], gmlp_ln_w[l],
                         gmlp_ws[l], gmlp_bs[l], gmlp_out_w[l], w_out[l], norm2_w[l], w_up[l],
                         conv_w[l], conv_b[l], w_down[l])
    return rms_norm(x, final_norm_w)
```

```python
import contextlib
import numpy as np
from concourse.bass_utils import run_bass_kernel_spmd
import concourse.bass as bass
import concourse.mybir as mybir

F32 = mybir.dt.float32
BF16 = mybir.dt.bfloat16
AF = mybir.ActivationFunctionType
ALU = mybir.AluOpType
AX = mybir.AxisListType

ENGS = ["sync", "gpsimd", "scalar", "vector", "tensor"]


class Op:
    __slots__ = ("idx", "eng", "fn", "deps", "dma_tag", "dma_val", "sig", "has_dep")

    def __init__(self, idx, eng, fn, dma_tag):
        self.idx = idx
        self.eng = eng
        self.fn = fn
        self.deps = set()
        self.dma_tag = dma_tag
        self.dma_val = 0
        self.sig = 0
        self.has_dep = False


class Sched:
    def __init__(self, nc, serialize=False):
        self.nc = nc
        self.serialize = serialize
        self.last_compute = None
        self.ops = []
        self.last_w = {}
        self.readers = {}
        self.dma_cnt = {}
        self.final_dma = []

    def add(self, eng, fn, reads=(), writes=(), dma_tag=None, final=False):
        op = Op(len(self.ops), eng, fn, dma_tag)
        deps = set()
        for r in reads:
            if r in self.last_w:
                deps.add(self.last_w[r])
        for w in writes:
            if w in self.last_w:
                deps.add(self.last_w[w])
            deps |= self.readers.get(w, set())
        for r in reads:
            self.readers.setdefault(r, set()).add(op)
        for w in writes:
            self.last_w[w] = op
            self.readers[w] = set()
        if self.serialize and dma_tag is None:
            if self.last_compute is not None:
                deps.add(self.last_compute)
            self.last_compute = op
        deps.discard(op)
        op.deps = deps
        for d in deps:
            d.has_dep = True
        if dma_tag is not None:
            self.dma_cnt[dma_tag] = self.dma_cnt.get(dma_tag, 0) + 16
            op.dma_val = self.dma_cnt[dma_tag]
        if final:
            op.has_dep = True
            self.final_dma.append(op)
        self.ops.append(op)
        return op

    def emit(self):
        nc = self.nc
        cnt = {e: 0 for e in ENGS}
        for op in self.ops:
            if op.dma_tag is None and op.has_dep:
                cnt[op.eng] += 1
                op.sig = cnt[op.eng]
        tags = sorted(self.dma_cnt.keys())
        import contextlib
        with contextlib.ExitStack() as st:
            csem = {e: st.enter_context(nc.semaphore("cs_" + e)) for e in ENGS}
            dsem = {t: st.enter_context(nc.semaphore("ds_%d" % i)) for i, t in enumerate(tags)}
            block = st.enter_context(nc.Block())
            ops = self.ops
            final_dma = self.final_dma

            def run(engname, e):
                waited_c = {x: 0 for x in ENGS}
                waited_d = {}
                for op in ops:
                    if op.eng != engname:
                        continue
                    need_c = {}
                    need_d = {}
                    for d in op.deps:
                        if d.dma_tag is not None:
                            need_d[d.dma_tag] = max(need_d.get(d.dma_tag, 0), d.dma_val)
                        else:
                            if d.eng == engname and engname == "tensor":
                                continue
                            need_c[d.eng] = max(need_c.get(d.eng, 0), d.sig)
                    for pe, v in need_c.items():
                        if v > waited_c[pe]:
                            e.wait_ge(csem[pe], v)
                            waited_c[pe] = v
                    for t, v in need_d.items():
                        if v > waited_d.get(t, 0):
                            e.wait_ge(dsem[t], v)
                            waited_d[t] = v
                    ins = op.fn(e)
                    if op.dma_tag is not None:
                        ins.then_inc(dsem[op.dma_tag], 16)
                    elif op.has_dep:
                        ins.then_inc(csem[engname], 1)
                if engname == "sync":
                    need_d = {}
                    for d in final_dma:
                        need_d[d.dma_tag] = max(need_d.get(d.dma_tag, 0), d.dma_val)
                    for t, v in need_d.items():
                        e.wait_ge(dsem[t], v)

            @block.sync
            def _(e):
                run("sync", e)

            @block.gpsimd
            def _(e):
                run("gpsimd", e)

            @block.scalar
            def _(e):
                run("scalar", e)

            @block.vector
            def _(e):
                run("vector", e)

            @block.tensor
            def _(e):
                run("tensor", e)


T = 1024
DM = 2048
INW = 6400
EPS = 1e-6
GC = 0.7978845608028654


def gelu_ops(S, nm, src_key, src_ap, dst_key, dst_ap, tmp_aps, tmp_keys):
    u, t = tmp_aps
    uk, tk = tmp_keys
    S.add("scalar", lambda e: e.activation(out=u, in_=src_ap, func=AF.Square), reads=[src_key], writes=[uk])
    S.add("vector", lambda e: e.tensor_scalar(out=u, in0=u, scalar1=0.044715, scalar2=1.0, op0=ALU.mult, op1=ALU.add),
          reads=[uk], writes=[uk])
    S.add("vector", lambda e: e.tensor_tensor(out=t, in0=u, in1=src_ap, op=ALU.mult), reads=[uk, src_key], writes=[tk])
    S.add("scalar", lambda e: e.activation(out=t, in_=t, func=AF.Sigmoid, scale=2.0 * GC), reads=[tk], writes=[tk])
    S.add("vector", lambda e: e.tensor_tensor(out=dst_ap, in0=t, in1=src_ap, op=ALU.mult), reads=[tk, src_key], writes=[dst_key])


def build_A():
    nc = bass.Bass("TRN2", target_bir_lowering=False)
    xT = nc.dram_tensor("xT", [DM, T], F32, kind="ExternalInput").ap()
    n1w = nc.dram_tensor("n1w", [128, 16], F32, kind="ExternalInput").ap()
    lnw = nc.dram_tensor("lnw", [128, 4], F32, kind="ExternalInput").ap()
    w_in = nc.dram_tensor("w_in", [DM, INW], F32, kind="ExternalInput").ap()
    projT = nc.dram_tensor("projT", [INW, T], F32, kind="ExternalOutput").ap()
    xv = xT.rearrange("(kc p) t -> p kc t", p=128)
    wv = w_in.rearrange("(kc p) n -> p kc n", p=128)
    WB = 640
    NWB = INW // WB
    with contextlib.ExitStack() as st:
        def sb(name, shape, dt):
            return st.enter_context(nc.sbuf_tensor(name, shape, dt))
        x_sb = sb("x_sb", [128, 16, T], F32)
        hT = sb("hT", [128, 16, T], BF16)
        sq = sb("sq", [128, 2, 512], BF16)
        ones = sb("ones", [128, 128], BF16)
        n1 = sb("n1", [128, 16], F32)
        ln = sb("ln", [128, 4], F32)
        rstd = sb("rstd", [128, T], F32)
        wb = [sb("wb%d" % i, [128, 16, WB], BF16) for i in range(3)]
        stg = [sb("stg%d" % i, [128, T], F32) for i in range(3)]
        gvb = sb("gvb", [128, 4, T], BF16)
        gsq = sb("gsq", [128, 4, T], BF16)
        tu = sb("tu", [128, 512], F32)
        tt = sb("tt", [128, 512], F32)
        mu = sb("mu", [128, T], F32)
        lrs = sb("lrs", [128, T], F32)
        epsb = sb("epsb", [128, 1], F32)
        gv = x_sb
        ps = [st.enter_context(nc.psum_tensor("ps%d" % i, [128, 512], F32)) for i in range(8)]
        S = Sched(nc)
        S.add("vector", lambda e: e.memset(ones[:], 1.0), writes=["ones"])
        S.add("vector", lambda e: e.memset(epsb[:], EPS), writes=["epsb"])
        S.add("sync", lambda e: e.dma_start(out=n1[:], in_=n1w), writes=["n1"], dma_tag="n1")
        S.add("sync", lambda e: e.dma_start(out=ln[:], in_=lnw), writes=["ln"], dma_tag="ln")
        for q in range(4):
            S.add("sync", lambda e, q=q: e.dma_start(out=x_sb[:, 4 * q:4 * q + 4, :], in_=xv[:, 4 * q:4 * q + 4, :]),
                  writes=[("x", kc) for kc in range(4 * q, 4 * q + 4)], dma_tag="x%d" % q)
        def load_w(j):
            s = j % 3
            for h in range(2):
                S.add("gpsimd", lambda e, j=j, s=s, h=h: e.dma_start(out=wb[s][:, 8 * h:8 * h + 8, :], in_=wv[:, 8 * h:8 * h + 8, j * WB:(j + 1) * WB]),
                      writes=[("w", s, h)], dma_tag="w%d" % s)
        load_w(0)
        load_w(1)
        for tb in range(2):
            tsl = slice(tb * 512, (tb + 1) * 512)
            for kc in range(16):
                b = kc % 2
                S.add("scalar", lambda e, kc=kc, b=b, tsl=tsl: e.activation(out=sq[:, b, :], in_=x_sb[:, kc, tsl], func=AF.Square),
                      reads=[("x", kc)], writes=[("sq", b)])
                S.add("tensor", lambda e, kc=kc, b=b, tb=tb: e.matmul(ps[6 + tb][:], lhsT=ones[:], rhs=sq[:, b, :], start=(kc == 0), stop=(kc == 15)),
                      reads=["ones", ("sq", b)], writes=[("ps", 6 + tb)])
            S.add("scalar", lambda e, tb=tb, tsl=tsl: e.activation(out=rstd[:, tsl], in_=ps[6 + tb][:], func=AF.Sqrt, scale=1.0 / DM, bias=epsb[:, 0:1]),
                  reads=[("ps", 6 + tb), "epsb"], writes=[("rstd", tb)])
            S.add("vector", lambda e, tsl=tsl: e.reciprocal(out=rstd[:, tsl], in_=rstd[:, tsl]), reads=[("rstd", tb)], writes=[("rstd", tb)])
            for kc in range(16):
                S.add("vector", lambda e, kc=kc, tsl=tsl: e.scalar_tensor_tensor(out=hT[:, kc, tsl], in0=x_sb[:, kc, tsl], scalar=n1[:, kc:kc + 1],
                                                                               in1=rstd[:, tsl], op0=ALU.mult, op1=ALU.mult),
                      reads=[("x", kc), "n1", ("rstd", tb)], writes=[("hT", kc, tb)])
        pi = 0
        si = 0
        for j in range(NWB):
            s = j % 3
            if j + 2 < NWB:
                load_w(j + 2)
            for mm in range(WB // 128):
                m = j * (WB // 128) + mm
                sg = si % 3
                si += 1
                for tb in range(2):
                    tsl = slice(tb * 512, (tb + 1) * 512)
                    p = pi % 6
                    pi += 1
                    for kc in range(16):
                        S.add("tensor", lambda e, p=p, s=s, kc=kc, mm=mm, tsl=tsl: e.matmul(ps[p][:], lhsT=wb[s][:, kc, mm * 128:(mm + 1) * 128],
                                                                                         rhs=hT[:, kc, tsl], start=(kc == 0), stop=(kc == 15)),
                              reads=[("w", s, kc // 8), ("hT", kc, tb)], writes=[("ps", p)])
                    if m < 42:
                        if tb == 0:
                            S.add("scalar", lambda e, p=p, sg=sg, tsl=tsl: e.copy(out=stg[sg][:, tsl], in_=ps[p][:]),
                                  reads=[("ps", p)], writes=[("stg", sg, tb)])
                        else:
                            S.add("vector", lambda e, p=p, sg=sg, tsl=tsl: e.tensor_copy(out=stg[sg][:, tsl], in_=ps[p][:]),
                                  reads=[("ps", p)], writes=[("stg", sg, tb)])
                    elif m < 46:
                        gelu_ops(S, "gu", ("ps", p), ps[p][:], ("stg", sg, tb), stg[sg][:, tsl], (tu[:], tt[:]), ("tu", "tt"))
                    else:
                        c = m - 46
                        gelu_ops(S, "gv", ("ps", p), ps[p][:], ("x", c), gv[:, c, tsl], (tu[:], tt[:]), ("tu", "tt"))
                        S.add("vector", lambda e, c=c, tsl=tsl: e.tensor_copy(out=gvb[:, c, tsl], in_=gv[:, c, tsl]),
                              reads=[("x", c)], writes=[("gvb", c, tb)])
                        S.add("scalar", lambda e, c=c, tsl=tsl: e.activation(out=gsq[:, c, tsl], in_=gv[:, c, tsl], func=AF.Square),
                              reads=[("x", c)], writes=[("gsq", c, tb)])
                if m < 46:
                    S.add("sync", lambda e, m=m, sg=sg: e.dma_start(out=projT[m * 128:(m + 1) * 128, :], in_=stg[sg][:]),
                          reads=[("stg", sg, 0), ("stg", sg, 1)], dma_tag="st%d" % sg, final=True)
        for tb in range(2):
            tsl = slice(tb * 512, (tb + 1) * 512)
            for c in range(4):
                S.add("tensor", lambda e, c=c, tsl=tsl: e.matmul(ps[0][:], lhsT=ones[:], rhs=gvb[:, c, tsl], start=(c == 0), stop=(c == 3)),
                      reads=["ones", ("gvb", c, tb)], writes=[("ps", 0)])
            for c in range(4):
                S.add("tensor", lambda e, c=c, tsl=tsl: e.matmul(ps[1][:], lhsT=ones[:], rhs=gsq[:, c, tsl], start=(c == 0), stop=(c == 3)),
                      reads=["ones", ("gsq", c, tb)], writes=[("ps", 1)])
            S.add("scalar", lambda e, tsl=tsl: e.mul(out=mu[:, tsl], in_=ps[0][:], mul=1.0 / 512), reads=[("ps", 0)], writes=[("mu", tb)])
            S.add("vector", lambda e, tsl=tsl: e.tensor_tensor(out=tu[:], in0=mu[:, tsl], in1=mu[:, tsl], op=ALU.mult), reads=[("mu", tb)], writes=["tu"])
            S.add("vector", lambda e, tsl=tsl: e.scalar_tensor_tensor(out=tt[:], in0=ps[1][:], scalar=1.0 / 512, in1=tu[:], op0=ALU.mult, op1=ALU.subtract),
                  reads=[("ps", 1), "tu"], writes=["tt"])
            S.add("scalar", lambda e, tsl=tsl: e.activation(out=lrs[:, tsl], in_=tt[:], func=AF.Sqrt, bias=epsb[:, 0:1]), reads=["tt", "epsb"], writes=[("lrs", tb)])
            S.add("vector", lambda e, tsl=tsl: e.reciprocal(out=lrs[:, tsl], in_=lrs[:, tsl]), reads=[("lrs", tb)], writes=[("lrs", tb)])
            for c in range(4):
                sg = c % 3
                S.add("vector", lambda e, c=c, tsl=tsl: e.tensor_tensor(out=gv[:, c, tsl], in0=gv[:, c, tsl], in1=mu[:, tsl], op=ALU.subtract),
                      reads=[("x", c), ("mu", tb)], writes=[("x", c)])
                S.add("vector", lambda e, c=c, tsl=tsl: e.scalar_tensor_tensor(out=gv[:, c, tsl], in0=gv[:, c, tsl], scalar=ln[:, c:c + 1], in1=lrs[:, tsl],
                                                                            op0=ALU.mult, op1=ALU.mult),
                      reads=[("x", c), "ln", ("lrs", tb)], writes=[("x", c)])
        for c in range(4):
            m = 46 + c
            S.add("sync", lambda e, m=m, c=c: e.dma_start(out=projT[m * 128:(m + 1) * 128, :], in_=gv[:, c, :]),
                  reads=[("x", c)], dma_tag="stgv", final=True)
        S.emit()
    return nc


SEQ = 8192
EPS = 1e-6
SCL = 128 ** -0.5
DILS = (1, 4, 16)


def build_B():
    nc = bass.Bass("TRN2", target_bir_lowering=False)
    def din(name, shape):
        return nc.dram_tensor(name, shape, F32, kind="ExternalInput").ap()
    r_qk = din("r_qk", [4, 128, SEQ])
    r_cs = din("r_cs", [2, 128, SEQ])
    r_v = din("r_v", [SEQ, 128])
    r_g = din("r_g", [SEQ, 128])
    r_tab = din("r_tab", [128, 1152])
    r_sc = din("r_sc", [128, 4])
    cmask = din("cmask", [128, 128])
    ident_d = din("ident", [128, 128])
    a_q = din("a_q", [3, 128, SEQ])
    a_k = din("a_k", [3, 128, SEQ])
    a_v = din("a_v", [SEQ, 128])
    a_mask = din("a_mask", [2, 128, 256])
    gm_u = din("gm_u", [4096, 128])
    gm_v = din("gm_v", [4096, 128])
    gm_wT = din("gm_wT", [128, 128])
    gm_bs = din("gm_bs", [128, 1])
    ret_o = nc.dram_tensor("ret_o", [SEQ, 128], F32, kind="ExternalOutput").ap()
    att_o = nc.dram_tensor("att_o", [SEQ, 128], F32, kind="ExternalOutput").ap()
    gm_o = nc.dram_tensor("gm_o", [4096, 128], F32, kind="ExternalOutput").ap()
    scr = nc.dram_tensor("scr", [3, SEQ, 130], F32).ap()
    with contextlib.ExitStack() as st:
        def sb(name, shape, dt):
            return st.enter_context(nc.sbuf_tensor(name, shape, dt))
        qt = sb("qt", [128, SEQ], BF16)
        kt = sb("kt", [128, SEQ], BF16)
        vb = sb("vb", [128, 64, 128], BF16)
        rin = [sb("rin%d" % i, [128, 6, 512], F32) for i in range(2)]
        rt1 = sb("rt1", [128, 512], F32)
        rt2 = sb("rt2", [128, 512], F32)
        tab = sb("tab", [128, 1152], F32)
        sc = sb("sc", [128, 4], F32)
        cm = sb("cm", [128, 128], F32)
        idf = sb("idf", [128, 128], F32)
        idb = sb("idb", [128, 128], BF16)
        gbuf = [sb("gbuf%d" % i, [128, 8, 128], F32) for i in range(2)]
        obuf = [sb("obuf%d" % i, [128, 8, 128], F32) for i in range(2)]
        ktok = [sb("ktok%d" % i, [128, 128], BF16) for i in range(2)]
        smb = [sb("smb%d" % i, [128, 128], BF16) for i in range(2)]
        Tst = sb("Tst", [128, 128], F32)
        Sbf = sb("Sbf", [128, 128], BF16)
        junk = sb("junk", [128, 256], F32)
        ssq = [sb("ssq%d" % i, [128, 1], F32) for i in range(2)]
        epsb = sb("epsb", [128, 1], F32)
        yb = sb("yb", [128, 128], F32)
        sgb = sb("sgb", [128, 128], F32)
        am = sb("am", [128, 2, 256], F32)
        s_sb = [sb("s_sb%d" % i, [128, 256], F32) for i in range(2)]
        pb = [sb("pb%d" % i, [128, 256], BF16) for i in range(2)]
        ptb = [sb("ptb%d" % i, [128, 2, 128], BF16) for i in range(2)]
        mx = [sb("mx%d" % i, [128, 2], F32) for i in range(2)]
        nst = [sb("nst%d" % i, [128, 130], F32) for i in range(3)]
        mrg = [sb("mrg%d" % i, [128, 3, 130], F32) for i in range(2)]
        mw = [sb("mw%d" % i, [128, 8], F32) for i in range(2)]
        gu_sb = sb("gu_sb", [128, 32, 128], F32)
        gv_sb = sb("gv_sb", [128, 32, 128], BF16)
        gwf = sb("gwf", [128, 128], F32)
        gwb = sb("gwb", [128, 128], BF16)
        gbs = sb("gbs", [128, 1], F32)
        ps = [st.enter_context(nc.psum_tensor("ps%d" % i, [128, 512], F32)) for i in range(6)]
        psb = [st.enter_context(nc.psum_tensor("psb%d" % i, [128, 2, 128], BF16)) for i in range(2)]
        S = Sched(nc)
        pc = [0]

        def nps():
            p = pc[0] % 6
            pc[0] += 1
            return p
        S.add("sync", lambda e: e.dma_start(out=tab[:], in_=r_tab), writes=["tab"], dma_tag="c0")
        S.add("sync", lambda e: e.dma_start(out=sc[:], in_=r_sc), writes=["sc"], dma_tag="c1")
        S.add("sync", lambda e: e.dma_start(out=cm[:], in_=cmask), writes=["cm"], dma_tag="c2")
        S.add("sync", lambda e: e.dma_start(out=idf[:], in_=ident_d), writes=["idf"], dma_tag="c3")
        S.add("sync", lambda e: e.dma_start(out=am[:], in_=a_mask.rearrange("m p k -> p m k")), writes=["am"], dma_tag="c4")
        S.add("sync", lambda e: e.dma_start(out=gwf[:], in_=gm_wT), writes=["gwf"], dma_tag="c5")
        S.add("sync", lambda e: e.dma_start(out=gbs[:], in_=gm_bs), writes=["gbs"], dma_tag="c6")
        S.add("vector", lambda e: e.tensor_copy(out=idb[:], in_=idf[:]), reads=["idf"], writes=["idb"])
        S.add("vector", lambda e: e.memset(epsb[:], EPS), writes=["epsb"])
        S.add("vector", lambda e: e.memset(Tst[:], 0.0), writes=["T"])
        S.add("vector", lambda e: e.memset(Sbf[:], 0.0), writes=["Sbf"])
        S.add("gpsimd", lambda e: e.dma_start(out=vb[:], in_=r_v.rearrange("(n p) e -> p n e", p=128)), writes=[("vbk", i) for i in range(64)], dma_tag="vb")
        def rot_block(bi):
            b = bi % 2
            tsl = slice(bi * 512, (bi + 1) * 512)
            S.add("sync", lambda e, b=b, tsl=tsl: e.dma_start(out=rin[b][:, 0:4, :], in_=r_qk[:, :, tsl].rearrange("f p t -> p f t")),
                  writes=[("rin", b, 0)], dma_tag="rin%d_0" % b)
            S.add("sync", lambda e, b=b, tsl=tsl: e.dma_start(out=rin[b][:, 4:6, :], in_=r_cs[:, :, tsl].rearrange("f p t -> p f t")),
                  writes=[("rin", b, 1)], dma_tag="rin%d_1" % b)
            rk = [("rin", b, 0), ("rin", b, 1)]
            for (i0, dst, dkey, toff) in ((0, qt, "qt", 0), (2, kt, "kt", 512)):
                S.add("vector", lambda e, b=b, i0=i0: e.tensor_tensor(out=rt1[:], in0=rin[b][:, i0, :], in1=rin[b][:, 4, :], op=ALU.mult), reads=rk, writes=["rt1"])
                S.add("gpsimd", lambda e, b=b, i0=i0: e.tensor_tensor(out=rt2[:], in0=rin[b][:, i0 + 1, :], in1=rin[b][:, 5, :], op=ALU.mult), reads=rk, writes=["rt2"])
                S.add("vector", lambda e: e.tensor_tensor(out=rt1[:], in0=rt1[:], in1=rt2[:], op=ALU.add), reads=["rt1", "rt2"], writes=["rt1"])
                S.add("vector", lambda e, dst=dst, tsl=tsl, toff=toff: e.tensor_tensor(out=dst[:, tsl], in0=rt1[:], in1=tab[:, toff:toff + 512], op=ALU.mult),
                      reads=["rt1", "tab"], writes=[(dkey, bi)])
        rot_block(0)
        oi = 0
        for n in range(64):
            bi = n // 4
            if n % 4 == 0 and bi + 1 < 16:
                rot_block(bi + 1)
            csl = slice(n * 128, (n + 1) * 128)
            kb = n % 2
            grp = n // 8
            gb = grp % 2
            if n % 8 == 0:
                S.add("sync", lambda e, gb=gb, grp=grp: e.dma_start(out=gbuf[gb][:], in_=r_g[grp * 1024:(grp + 1) * 1024, :].rearrange("(n p) e -> p n e", p=128)),
                      writes=[("gbuf", gb)], dma_tag="gbuf%d" % gb)
            S.add("tensor", lambda e, kb=kb, csl=csl: e.transpose(out=psb[kb][:, 0, :], in_=kt[:, csl], identity=idb[:]), reads=[("kt", bi), "idb"], writes=[("psb", kb)])
            S.add("scalar", lambda e, kb=kb: e.copy(out=ktok[kb][:], in_=psb[kb][:, 0, :]), reads=[("psb", kb)], writes=[("ktok", kb)])
            p1 = nps()
            S.add("tensor", lambda e, p1=p1, csl=csl: e.matmul(ps[p1][:, 0:128], lhsT=kt[:, csl], rhs=qt[:, csl], start=True, stop=True),
                  reads=[("kt", bi), ("qt", bi)], writes=[("ps", p1)])
            S.add("vector", lambda e, p1=p1, kb=kb: e.tensor_tensor(out=smb[kb][:], in0=ps[p1][:, 0:128], in1=cm[:], op=ALU.mult), reads=[("ps", p1), "cm"], writes=[("smb", kb)])
            p2 = nps()
            S.add("tensor", lambda e, p2=p2, kb=kb, n=n: e.matmul(ps[p2][:, 0:128], lhsT=smb[kb][:], rhs=vb[:, n, :], start=True, stop=False),
                  reads=[("smb", kb), ("vbk", n)], writes=[("ps", p2)])
            S.add("tensor", lambda e, p2=p2, csl=csl: e.matmul(ps[p2][:, 0:128], lhsT=qt[:, csl], rhs=Sbf[:], start=False, stop=True),
                  reads=[("qt", bi), "Sbf"], writes=[("ps", p2)])
            p3 = nps()
            S.add("tensor", lambda e, p3=p3, kb=kb, n=n: e.matmul(ps[p3][:, 0:128], lhsT=ktok[kb][:], rhs=vb[:, n, :], start=True, stop=True),
                  reads=[("ktok", kb), ("vbk", n)], writes=[("ps", p3)])
            S.add("vector", lambda e, p3=p3: e.scalar_tensor_tensor(out=Tst[:], in0=Tst[:], scalar=sc[:, 0:1], in1=ps[p3][:, 0:128], op0=ALU.mult, op1=ALU.add),
                  reads=["T", "sc", ("ps", p3)], writes=["T"])
            S.add("scalar", lambda e: e.activation(out=Sbf[:], in_=Tst[:], func=AF.Identity, scale=sc[:, 0:1]), reads=["T", "sc"], writes=["Sbf"])
            sq_i = n % 2
            S.add("vector", lambda e, sq_i=sq_i: e.memset(ssq[sq_i][:], 0.0), writes=[("ssq", sq_i)])
            S.add("scalar", lambda e, p2=p2, sq_i=sq_i: e.activation(out=junk[:, 0:128], in_=ps[p2][:, 0:128], func=AF.Square, accum_out=ssq[sq_i][:]),
                  reads=[("ps", p2), ("ssq", sq_i)], writes=["junk", ("ssq", sq_i)])
            S.add("scalar", lambda e, sq_i=sq_i: e.activation(out=ssq[sq_i][:], in_=ssq[sq_i][:], func=AF.Sqrt, scale=1.0 / 128, bias=epsb[:, 0:1]),
                  reads=[("ssq", sq_i), "epsb"], writes=[("ssq", sq_i)])
            S.add("vector", lambda e, sq_i=sq_i: e.reciprocal(out=ssq[sq_i][:], in_=ssq[sq_i][:]), reads=[("ssq", sq_i)], writes=[("ssq", sq_i)])
            S.add("vector", lambda e, p2=p2, sq_i=sq_i: e.scalar_tensor_tensor(out=yb[:], in0=ps[p2][:, 0:128], scalar=ssq[sq_i][:, 0:1], in1=tab[:, 1024:1152], op0=ALU.mult, op1=ALU.mult),
                  reads=[("ps", p2), ("ssq", sq_i), "tab"], writes=["yb"])
            S.add("scalar", lambda e, gb=gb, n=n: e.activation(out=sgb[:], in_=gbuf[gb][:, n % 8, :], func=AF.Silu), reads=[("gbuf", gb)], writes=["sgb"])
            S.add("vector", lambda e, gb=gb, n=n: e.tensor_tensor(out=obuf[gb][:, n % 8, :], in0=yb[:], in1=sgb[:], op=ALU.mult), reads=["yb", "sgb"], writes=[("obuf", gb, n % 8)])
            if n % 8 == 7:
                S.add("sync", lambda e, gb=gb, grp=grp: e.dma_start(out=ret_o[grp * 1024:(grp + 1) * 1024, :].rearrange("(n p) e -> p n e", p=128), in_=obuf[gb][:]),
                      reads=[("obuf", gb, i) for i in range(8)], dma_tag="ro%d" % gb, final=True)
        bi_ctr = 0
        for pi_, d in enumerate(DILS):
            L = SEQ // d
            nb = L // 128
            for h2 in range(2):
                S.add("gpsimd", lambda e, pi_=pi_, h2=h2: e.dma_start(out=qt[:, h2 * 4096:(h2 + 1) * 4096], in_=a_q[pi_, :, h2 * 4096:(h2 + 1) * 4096]),
                      writes=[("qt", i) for i in range(8 * h2, 8 * h2 + 8)], dma_tag="aq%d" % h2)
                S.add("gpsimd", lambda e, pi_=pi_, h2=h2: e.dma_start(out=kt[:, h2 * 4096:(h2 + 1) * 4096], in_=a_k[pi_, :, h2 * 4096:(h2 + 1) * 4096]),
                      writes=[("kt", i) for i in range(8 * h2, 8 * h2 + 8)], dma_tag="ak%d" % h2)
            vview = a_v.rearrange("(b p d) e -> d p b e", p=128, d=d)
            for r in range(d):
                S.add("gpsimd", lambda e, r=r, nb=nb, vview=vview: e.dma_start(out=vb[:, r * nb:(r + 1) * nb, :], in_=vview[r]),
                      writes=[("vbk", i) for i in range(r * nb, (r + 1) * nb)], dma_tag="av%d" % r)
            for r in range(d):
                for b in range(nb):
                    B = r * nb + b
                    i2 = bi_ctr % 2
                    i3 = bi_ctr % 3
                    bi_ctr += 1
                    q0 = B * 128
                    k0 = (B - 1) * 128 if b > 0 else B * 128
                    mi = 0 if b > 0 else 1
                    hq = (q0 // 4096)
                    rd = [("qt", q0 // 512), ("kt", q0 // 512), ("kt", k0 // 512), ("kt", (k0 + 255) // 512)]
                    p1 = nps()
                    if b > 0:
                        S.add("tensor", lambda e, p1=p1, q0=q0, k0=k0: e.matmul(ps[p1][:, 0:256], lhsT=qt[:, q0:q0 + 128], rhs=kt[:, k0:k0 + 256], start=True, stop=True),
                              reads=rd, writes=[("ps", p1)])
                    else:
                        S.add("tensor", lambda e, p1=p1, q0=q0: e.matmul(ps[p1][:, 0:128], lhsT=qt[:, q0:q0 + 128], rhs=kt[:, q0:q0 + 128], start=True, stop=True),
                              reads=rd, writes=[("ps", p1)])
                        S.add("tensor", lambda e, p1=p1, q0=q0: e.matmul(ps[p1][:, 128:256], lhsT=qt[:, q0:q0 + 128], rhs=kt[:, q0:q0 + 128], start=True, stop=True),
                              reads=rd, writes=[("ps", p1)])
                    S.add("vector", lambda e, p1=p1, i2=i2, mi=mi: e.tensor_tensor(out=s_sb[i2][:], in0=ps[p1][:, 0:256], in1=am[:, mi, :], op=ALU.add),
                          reads=[("ps", p1), "am"], writes=[("s_sb", i2)])
                    S.add("vector", lambda e, i2=i2: e.reduce_max(out=mx[i2][:, 0:1], in_=s_sb[i2][:], axis=AX.X), reads=[("s_sb", i2)], writes=[("mx", i2)])
                    S.add("vector", lambda e, i2=i2: e.tensor_scalar(out=mx[i2][:, 1:2], in0=mx[i2][:, 0:1], scalar1=-SCL, scalar2=None, op0=ALU.mult),
                          reads=[("mx", i2)], writes=[("mx", i2)])
                    S.add("vector", lambda e, i3=i3: e.memset(nst[i3][:, 128:129], 0.0), writes=[("nst", i3, 1)])
                    S.add("scalar", lambda e, i2=i2, i3=i3: e.activation(out=pb[i2][:], in_=s_sb[i2][:], func=AF.Exp, scale=SCL, bias=mx[i2][:, 1:2], accum_out=nst[i3][:, 128:129]),
                          reads=[("s_sb", i2), ("mx", i2), ("nst", i3, 1)], writes=[("pb", i2), ("nst", i3, 1)])
                    S.add("vector", lambda e, i2=i2, i3=i3: e.tensor_scalar(out=nst[i3][:, 129:130], in0=mx[i2][:, 0:1], scalar1=SCL, scalar2=None, op0=ALU.mult),
                          reads=[("mx", i2)], writes=[("nst", i3, 2)])
                    for hh in range(2):
                        S.add("tensor", lambda e, i2=i2, hh=hh: e.transpose(out=psb[i2][:, hh, :], in_=pb[i2][:, hh * 128:(hh + 1) * 128], identity=idb[:]),
                              reads=[("pb", i2), "idb"], writes=[("psb", i2)])
                    S.add("scalar", lambda e, i2=i2: e.copy(out=ptb[i2][:], in_=psb[i2][:]), reads=[("psb", i2)], writes=[("ptb", i2)])
                    p2 = nps()
                    vprev = B - 1 if b > 0 else B
                    S.add("tensor", lambda e, p2=p2, i2=i2, vprev=vprev: e.matmul(ps[p2][:, 0:128], lhsT=ptb[i2][:, 0, :], rhs=vb[:, vprev, :], start=True, stop=False),
                          reads=[("ptb", i2), ("vbk", vprev), ("vbk", B)], writes=[("ps", p2)])
                    S.add("tensor", lambda e, p2=p2, i2=i2, B=B: e.matmul(ps[p2][:, 0:128], lhsT=ptb[i2][:, 1, :], rhs=vb[:, B, :], start=False, stop=True),
                          reads=[("ptb", i2), ("vbk", vprev), ("vbk", B)], writes=[("ps", p2)])
                    S.add("scalar", lambda e, p2=p2, i3=i3: e.copy(out=nst[i3][:, 0:128], in_=ps[p2][:, 0:128]), reads=[("ps", p2)], writes=[("nst", i3, 0)])
                    rows = scr[pi_].rearrange("(b p d) c -> d b p c", p=128, d=d)[r, b]
                    S.add("sync", lambda e, i3=i3, rows=rows: e.dma_start(out=rows, in_=nst[i3][:]),
                          reads=[("nst", i3, 0), ("nst", i3, 1), ("nst", i3, 2)], writes=[("scr", pi_, r, b)], dma_tag="nst%d" % i3)
        for nbk in range(64):
            i2 = nbk % 2
            keys = [("scr", 0, 0, nbk)] + [("scr", 1, r, nbk // 4) for r in range(4)] + [("scr", 2, r, nbk // 16) for r in range(16)]
            S.add("sync", lambda e, i2=i2, nbk=nbk: e.dma_start(out=mrg[i2][:], in_=scr[:, nbk * 128:(nbk + 1) * 128, :].rearrange("f p c -> p f c")),
                  reads=keys, writes=[("mrg", i2)], dma_tag="mrg%d" % i2)
            mk = [("mrg", i2)]
            w = mw[i2]
            S.add("vector", lambda e, i2=i2, w=w: e.tensor_tensor(out=w[:, 0:1], in0=mrg[i2][:, 0, 129:130], in1=mrg[i2][:, 1, 129:130], op=ALU.max), reads=mk, writes=[("mw", i2)])
            S.add("vector", lambda e, i2=i2, w=w: e.tensor_tensor(out=w[:, 0:1], in0=w[:, 0:1], in1=mrg[i2][:, 2, 129:130], op=ALU.max), reads=mk + [("mw", i2)], writes=[("mw", i2)])
            S.add("vector", lambda e, w=w: e.tensor_scalar(out=w[:, 1:2], in0=w[:, 0:1], scalar1=-1.0, scalar2=None, op0=ALU.mult), reads=[("mw", i2)], writes=[("mw", i2)])
            S.add("scalar", lambda e, i2=i2, w=w: e.activation(out=w[:, 2:5], in_=mrg[i2][:, :, 129], func=AF.Exp, bias=w[:, 1:2]), reads=mk + [("mw", i2)], writes=[("mw", i2)])
            S.add("vector", lambda e, i2=i2, w=w: e.tensor_tensor(out=w[:, 5:8], in0=w[:, 2:5], in1=mrg[i2][:, :, 128], op=ALU.mult), reads=mk + [("mw", i2)], writes=[("mw", i2)])
            S.add("vector", lambda e, w=w: e.reduce_sum(out=w[:, 0:1], in_=w[:, 5:8], axis=AX.X), reads=[("mw", i2)], writes=[("mw", i2)])
            S.add("vector", lambda e, w=w: e.reciprocal(out=w[:, 0:1], in_=w[:, 0:1]), reads=[("mw", i2)], writes=[("mw", i2)])
            S.add("vector", lambda e, w=w: e.tensor_scalar(out=w[:, 2:5], in0=w[:, 2:5], scalar1=w[:, 0:1], scalar2=None, op0=ALU.mult), reads=[("mw", i2)], writes=[("mw", i2)])
            ob = s_sb[i2]
            S.add("vector", lambda e, i2=i2, w=w, ob=ob: e.tensor_scalar(out=ob[:, 0:128], in0=mrg[i2][:, 0, 0:128], scalar1=w[:, 2:3], scalar2=None, op0=ALU.mult),
                  reads=mk + [("mw", i2)], writes=[("s_sb", i2)])
            for p_ in (1, 2):
                S.add("vector", lambda e, i2=i2, w=w, ob=ob, p_=p_: e.scalar_tensor_tensor(out=ob[:, 0:128], in0=mrg[i2][:, p_, 0:128], scalar=w[:, 2 + p_:3 + p_], in1=ob[:, 0:128],
                                                                                     op0=ALU.mult, op1=ALU.add),
                      reads=mk + [("mw", i2), ("s_sb", i2)], writes=[("s_sb", i2)])
            S.add("sync", lambda e, ob=ob, nbk=nbk: e.dma_start(out=att_o[nbk * 128:(nbk + 1) * 128, :], in_=ob[:, 0:128]), reads=[("s_sb", i2)], dma_tag="ao%d" % i2, final=True)
        S.add("vector", lambda e: e.tensor_tensor(out=gwb[:], in0=gwf[:], in1=cm[:], op=ALU.mult), reads=["gwf", "cm"], writes=["gwb"])
        S.add("sync", lambda e: e.dma_start(out=gu_sb[:], in_=gm_u.rearrange("(n p) e -> p n e", p=128)), writes=["gu"], dma_tag="gu")
        S.add("gpsimd", lambda e: e.dma_start(out=gv_sb[:], in_=gm_v.rearrange("(n p) e -> p n e", p=128)), writes=["gv"], dma_tag="gv")
        for n in range(32):
            p1 = nps()
            gb = (n // 8) % 2
            S.add("tensor", lambda e, p1=p1, n=n: e.matmul(ps[p1][:, 0:128], lhsT=gwb[:], rhs=gv_sb[:, n, :], start=True, stop=True), reads=["gwb", "gv"], writes=[("ps", p1)])
            S.add("vector", lambda e, p1=p1, n=n, gb=gb: e.scalar_tensor_tensor(out=obuf[gb][:, n % 8, :], in0=ps[p1][:, 0:128], scalar=gbs[:, 0:1], in1=gu_sb[:, n, :], op0=ALU.add, op1=ALU.mult),
                  reads=[("ps", p1), "gbs", "gu"], writes=[("obuf", gb, n % 8)])
            if n % 8 == 7:
                g8 = n // 8
                S.add("sync", lambda e, gb=gb, g8=g8: e.dma_start(out=gm_o[g8 * 1024:(g8 + 1) * 1024, :].rearrange("(n p) e -> p n e", p=128), in_=obuf[gb][:]),
                      reads=[("obuf", gb, i) for i in range(8)], dma_tag="go%d" % gb, final=True)
        S.emit()
    return nc


TC = 1026
DM = 2048
DFF = 5632
EPS = 1e-6
TBS = [(0, 342), (342, 684), (684, 1026)]


def build_C(final=False):
    nc = bass.Bass("TRN2", target_bir_lowering=False)
    xT_d = nc.dram_tensor("xT", [DM, TC], F32, kind="ExternalInput").ap()
    mT_d = nc.dram_tensor("mT", [DM, TC], F32, kind="ExternalInput").ap()
    vecs_d = nc.dram_tensor("vecs", [128, 64], F32, kind="ExternalInput").ap()
    cw_d = nc.dram_tensor("cw", [128, 4, 88], F32, kind="ExternalInput").ap()
    w_out = nc.dram_tensor("w_out", [DM, DM], F32, kind="ExternalInput").ap()
    w_up = nc.dram_tensor("w_up", [DM, 2 * DFF], F32, kind="ExternalInput").ap()
    w_down = nc.dram_tensor("w_down", [DFF, DM], F32, kind="ExternalInput").ap()
    outT = nc.dram_tensor("outT", [DM, 1024], F32, kind="ExternalOutput").ap()
    xv = xT_d.rearrange("(kc p) t -> p kc t", p=128)
    mv = mT_d.rearrange("(kc p) t -> p kc t", p=128)
    wov = w_out.rearrange("(kc p) n -> p kc n", p=128)
    wuv = w_up.rearrange("(kc p) n -> p kc n", p=128)
    wdv = w_down.rearrange("(kc p) n -> p kc n", p=128)
    ov = outT.rearrange("(kc p) t -> p kc t", p=128)
    with contextlib.ExitStack() as st:
        def sb(name, shape, dt):
            return st.enter_context(nc.sbuf_tensor(name, shape, dt))
        x_sb = sb("x_sb", [128, 16, TC], F32)
        nT = sb("nT", [128, 16, TC], BF16)
        act = sb("act", [128, 12, 1024], BF16)
        wb = [sb("wb%d" % i, [128, 16, 512], BF16) for i in range(3)]
        sq = [sb("sq%d" % i, [128, TC], BF16) for i in range(2)]
        rs = [sb("rs%d" % i, [128, TC], F32) for i in range(2)]
        hg = sb("hg", [128, TC], F32)
        hv = sb("hv", [128, TC], F32)
        ag = sb("ag", [128, 1024], F32)
        av = sb("av", [128, 1024], F32)
        mst = [hg, hv]
        ones = sb("ones", [128, 128], BF16)
        vecs = sb("vecs_sb", [128, 64], F32)
        cw = sb("cw_sb", [128, 4, 88], F32)
        epsb = sb("epsb", [128, 1], F32)
        ps = [st.enter_context(nc.psum_tensor("ps%d" % i, [128, 512], F32)) for i in range(8)]
        S = Sched(nc, serialize=True)
        pc = [0]

        def nps():
            p = pc[0] % 8
            pc[0] += 1
            return p
        wc = [0]

        def load_w(view, r0, nr, c0):
            s = wc[0] % 3
            wc[0] += 1
            h = (nr + 1) // 2
            for a, b in ((0, h), (h, nr)):
                S.add("gpsimd", lambda e, s=s, a=a, b=b: e.dma_start(out=wb[s][:, a:b, :], in_=view[:, r0 + a:r0 + b, c0:c0 + 512]),
                      writes=[("w", s, 0), ("w", s, 1)] if False else [("w", s, 0 if a == 0 else 1)], dma_tag="w%d_%d" % (s, 0 if a == 0 else 1))
            return s, h

        def wkeys(s):
            return [("w", s, 0), ("w", s, 1)]

        S.add("vector", lambda e: e.memset(ones[:], 1.0), writes=["ones"])
        S.add("vector", lambda e: e.memset(epsb[:], EPS), writes=["epsb"])
        S.add("sync", lambda e: e.dma_start(out=vecs[:], in_=vecs_d), writes=["vecs"], dma_tag="vecs")
        S.add("sync", lambda e: e.dma_start(out=cw[:], in_=cw_d), writes=["cw"], dma_tag="cw")
        for q in range(4):
            S.add("sync", lambda e, q=q: e.dma_start(out=x_sb[:, 4 * q:4 * q + 4, 2:1026], in_=xv[:, 4 * q:4 * q + 4, 2:1026]),
                  writes=[("x", kc) for kc in range(4 * q, 4 * q + 4)], dma_tag="x%d" % q)
            S.add("sync", lambda e, q=q: e.dma_start(out=x_sb[:, 4 * q:4 * q + 4, 0:2], in_=xv[:, 4 * q:4 * q + 4, 0:2]),
                  writes=[("x", kc) for kc in range(4 * q, 4 * q + 4)], dma_tag="xh%d" % q)
        wq = []
        wq.append(load_w(wov, 0, 16, 0))
        wq.append(load_w(wov, 0, 16, 512))

        def stats(groups_src, nchunks, rsl, scale):
            pass

        mi = [0]

        def load_m(c):
            b = mi[0] % 2
            mi[0] += 1
            S.add("sync", lambda e, b=b, c=c: e.dma_start(out=mst[b][:, 2:1026], in_=mv[:, c, 2:1026]), writes=[("mst", b)], dma_tag="mst%d" % b)
            S.add("sync", lambda e, b=b, c=c: e.dma_start(out=mst[b][:, 0:2], in_=mv[:, c, 0:2]), writes=[("mst", b)], dma_tag="msth%d" % b)
            return b
        sqi = [0]
        for gi, (c0, c1) in enumerate(((6, 12), (12, 16))):
            pbs = [nps() for _ in range(3)]
            for c in range(c0, c1):
                b = load_m(c)
                q = sqi[0] % 2
                sqi[0] += 1
                S.add("scalar", lambda e, b=b, q=q: e.activation(out=sq[q][:], in_=mst[b][:], func=AF.Square), reads=[("mst", b)], writes=[("sq", q)])
                for ti, (t0, t1) in enumerate(TBS):
                    S.add("tensor", lambda e, q=q, ti=ti, t0=t0, t1=t1, c=c, p=pbs[ti]: e.matmul(ps[p][:, 0:342], lhsT=ones[:], rhs=sq[q][:, t0:t1],
                                                                                              start=(c == c0), stop=(c == c1 - 1)),
                          reads=["ones", ("sq", q)], writes=[("ps", pbs[ti])])
            for ti, (t0, t1) in enumerate(TBS):
                S.add("scalar", lambda e, gi=gi, ti=ti, t0=t0, t1=t1, p=pbs[ti], n=(c1 - c0) * 128: e.activation(out=rs[gi][:, t0:t1], in_=ps[p][:, 0:342], func=AF.Sqrt,
                                                                                                         scale=1.0 / n, bias=epsb[:, 0:1]),
                      reads=[("ps", pbs[ti]), "epsb"], writes=[("rs", gi)])
            S.add("vector", lambda e, gi=gi: e.reciprocal(out=rs[gi][:], in_=rs[gi][:]), reads=[("rs", gi)], writes=[("rs", gi)])
            S.add("vector", lambda e, gi=gi: e.tensor_scalar_min(out=rs[gi][:], in0=rs[gi][:], scalar1=1.0e4), reads=[("rs", gi)], writes=[("rs", gi)])
        for c in range(16):
            b = load_m(c)
            if c < 6:
                S.add("vector", lambda e, b=b, c=c: e.tensor_copy(out=nT[:, c, :], in_=mst[b][:]), reads=[("mst", b)], writes=[("nT", c)])
            else:
                gi = 0 if c < 12 else 1
                S.add("vector", lambda e, b=b, c=c, gi=gi: e.scalar_tensor_tensor(out=nT[:, c, :], in0=mst[b][:], scalar=vecs[:, c:c + 1], in1=rs[gi][:],
                                                                                 op0=ALU.mult, op1=ALU.mult),
                      reads=[("mst", b), "vecs", ("rs", gi)], writes=[("nT", c)])
        for jb in range(4):
            s, h = wq.pop(0)
            if jb + 2 < 4:
                wq.append(load_w(wov, 0, 16, (jb + 2) * 512))
            elif jb == 2:
                wq.append(load_w(wuv, 0, 16, 0))
            else:
                wq.append(load_w(wuv, 0, 16, DFF))
            for mm in range(4):
                mo = jb * 4 + mm
                for ti, (t0, t1) in enumerate(TBS):
                    p = nps()
                    for kc in range(16):
                        S.add("tensor", lambda e, p=p, s=s, kc=kc, mm=mm, t0=t0, t1=t1: e.matmul(ps[p][:, 0:342], lhsT=wb[s][:, kc, mm * 128:(mm + 1) * 128],
                                                                                              rhs=nT[:, kc, t0:t1], start=(kc == 0), stop=(kc == 15)),
                              reads=[("w", s, 0 if kc < h else 1), ("nT", kc)], writes=[("ps", p)])
                    S.add("vector", lambda e, p=p, mo=mo, t0=t0, t1=t1: e.tensor_tensor(out=x_sb[:, mo, t0:t1], in0=x_sb[:, mo, t0:t1], in1=ps[p][:, 0:342], op=ALU.add),
                          reads=[("ps", p), ("x", mo)], writes=[("x", mo)])
        pbs = [nps() for _ in range(3)]
        for kc in range(16):
            q = sqi[0] % 2
            sqi[0] += 1
            S.add("scalar", lambda e, kc=kc, q=q: e.activation(out=sq[q][:], in_=x_sb[:, kc, :], func=AF.Square), reads=[("x", kc)], writes=[("sq", q)])
            for ti, (t0, t1) in enumerate(TBS):
                S.add("tensor", lambda e, q=q, t0=t0, t1=t1, kc=kc, p=pbs[ti]: e.matmul(ps[p][:, 0:342], lhsT=ones[:], rhs=sq[q][:, t0:t1], start=(kc == 0), stop=(kc == 15)),
                      reads=["ones", ("sq", q)], writes=[("ps", pbs[ti])])
        for ti, (t0, t1) in enumerate(TBS):
            S.add("scalar", lambda e, t0=t0, t1=t1, p=pbs[ti]: e.activation(out=rs[0][:, t0:t1], in_=ps[p][:, 0:342], func=AF.Sqrt, scale=1.0 / DM, bias=epsb[:, 0:1]),
                  reads=[("ps", pbs[ti]), "epsb"], writes=[("rs", 0)])
        S.add("vector", lambda e: e.reciprocal(out=rs[0][:], in_=rs[0][:]), reads=[("rs", 0)], writes=[("rs", 0)])
        S.add("vector", lambda e: e.tensor_scalar_min(out=rs[0][:], in0=rs[0][:], scalar1=1.0e4), reads=[("rs", 0)], writes=[("rs", 0)])
        for kc in range(16):
            S.add("vector", lambda e, kc=kc: e.scalar_tensor_tensor(out=nT[:, kc, :], in0=x_sb[:, kc, :], scalar=vecs[:, 16 + kc:17 + kc], in1=rs[0][:],
                                                                    op0=ALU.mult, op1=ALU.mult),
                  reads=[("x", kc), "vecs", ("rs", 0)], writes=[("nT", kc)])
        groups = [(0, 3), (3, 6), (6, 9), (9, 11)]
        for (b0, b1) in groups:
            for jb in range(b0, b1):
                sg, hgk = wq.pop(0)
                sv, hvk = wq.pop(0)
                for mm in range(4):
                    ch = jb * 4 + mm
                    al = (jb - b0) * 4 + mm
                    for (s, hsplit, hb, hkey, chx) in ((sg, hgk, hg, "hg", ch), (sv, hvk, hv, "hv", 44 + ch)):
                        for ti, (t0, t1) in enumerate(TBS):
                            p = nps()
                            for kc in range(16):
                                S.add("tensor", lambda e, p=p, s=s, kc=kc, mm=mm, t0=t0, t1=t1: e.matmul(ps[p][:, 0:342], lhsT=wb[s][:, kc, mm * 128:(mm + 1) * 128],
                                                                                                      rhs=nT[:, kc, t0:t1], start=(kc == 0), stop=(kc == 15)),
                                      reads=[("w", s, 0 if kc < hsplit else 1), ("nT", kc)], writes=[("ps", p)])
                            S.add("scalar", lambda e, p=p, hb=hb, t0=t0, t1=t1: e.copy(out=hb[:, t0:t1], in_=ps[p][:, 0:342]), reads=[("ps", p)], writes=[(hkey, ti)])
                    for (hb, hkey, ab, akey, chx) in ((hg, "hg", ag, "ag", ch), (hv, "hv", av, "av", 44 + ch)):
                        hk = [(hkey, 0), (hkey, 1), (hkey, 2)]
                        S.add("scalar", lambda e, hb=hb, ab=ab, chx=chx: e.activation(out=ab[:], in_=hb[:, 2:1026], func=AF.Identity, scale=cw[:, 2, chx:chx + 1], bias=cw[:, 3, chx:chx + 1]),
                              reads=hk + ["cw"], writes=[akey])
                        S.add("vector", lambda e, hb=hb, ab=ab, chx=chx: e.scalar_tensor_tensor(out=ab[:], in0=hb[:, 1:1025], scalar=cw[:, 1, chx:chx + 1], in1=ab[:], op0=ALU.mult, op1=ALU.add),
                              reads=hk + ["cw", akey], writes=[akey])
                        S.add("vector", lambda e, hb=hb, ab=ab, chx=chx: e.scalar_tensor_tensor(out=ab[:], in0=hb[:, 0:1024], scalar=cw[:, 0, chx:chx + 1], in1=ab[:], op0=ALU.mult, op1=ALU.add),
                              reads=hk + ["cw", akey], writes=[akey])
                    S.add("scalar", lambda e: e.activation(out=ag[:], in_=ag[:], func=AF.Silu), reads=["ag"], writes=["ag"])
                    S.add("vector", lambda e, al=al: e.tensor_tensor(out=act[:, al, :], in0=ag[:], in1=av[:], op=ALU.mult), reads=["ag", "av"], writes=[("act", al)])
                if jb + 1 < b1:
                    wq.append(load_w(wuv, 0, 16, (jb + 1) * 512))
                    wq.append(load_w(wuv, 0, 16, DFF + (jb + 1) * 512))
            nch = (b1 - b0) * 4
            r0 = b0 * 4
            wq.append(load_w(wdv, r0, nch, 0))
            wq.append(load_w(wdv, r0, nch, 512))
            for cb in range(4):
                s, h = wq.pop(0)
                if cb + 2 < 4:
                    wq.append(load_w(wdv, r0, nch, (cb + 2) * 512))
                elif b1 < 11:
                    if cb == 2:
                        wq.append(load_w(wuv, 0, 16, b1 * 512))
                    else:
                        wq.append(load_w(wuv, 0, 16, DFF + b1 * 512))
                for mm in range(4):
                    mo = cb * 4 + mm
                    for t2 in range(2):
                        p = nps()
                        for i in range(nch):
                            S.add("tensor", lambda e, p=p, s=s, i=i, mm=mm, t2=t2: e.matmul(ps[p][:], lhsT=wb[s][:, i, mm * 128:(mm + 1) * 128],
                                                                                         rhs=act[:, i, t2 * 512:(t2 + 1) * 512], start=(i == 0), stop=(i == nch - 1)),
                                  reads=[("w", s, 0 if i < h else 1), ("act", i)], writes=[("ps", p)])
                        S.add("vector", lambda e, p=p, mo=mo, t2=t2: e.tensor_tensor(out=x_sb[:, mo, 2 + t2 * 512:2 + (t2 + 1) * 512], in0=x_sb[:, mo, 2 + t2 * 512:2 + (t2 + 1) * 512],
                                                                                    in1=ps[p][:], op=ALU.add),
                              reads=[("ps", p), ("x", mo)], writes=[("x", mo)])
        if final:
            pbs = [nps() for _ in range(3)]
            for kc in range(16):
                q = sqi[0] % 2
                sqi[0] += 1
                S.add("scalar", lambda e, kc=kc, q=q: e.activation(out=sq[q][:], in_=x_sb[:, kc, :], func=AF.Square), reads=[("x", kc)], writes=[("sq", q)])
                for ti, (t0, t1) in enumerate(TBS):
                    S.add("tensor", lambda e, q=q, t0=t0, t1=t1, kc=kc, p=pbs[ti]: e.matmul(ps[p][:, 0:342], lhsT=ones[:], rhs=sq[q][:, t0:t1], start=(kc == 0), stop=(kc == 15)),
                          reads=["ones", ("sq", q)], writes=[("ps", pbs[ti])])
            for ti, (t0, t1) in enumerate(TBS):
                S.add("scalar", lambda e, t0=t0, t1=t1, p=pbs[ti]: e.activation(out=rs[1][:, t0:t1], in_=ps[p][:, 0:342], func=AF.Sqrt, scale=1.0 / DM, bias=epsb[:, 0:1]),
                      reads=[("ps", pbs[ti]), "epsb"], writes=[("rs", 1)])
            S.add("vector", lambda e: e.reciprocal(out=rs[1][:], in_=rs[1][:]), reads=[("rs", 1)], writes=[("rs", 1)])
            for kc in range(16):
                S.add("vector", lambda e, kc=kc: e.scalar_tensor_tensor(out=x_sb[:, kc, :], in0=x_sb[:, kc, :], scalar=vecs[:, 32 + kc:33 + kc], in1=rs[1][:],
                                                                        op0=ALU.mult, op1=ALU.mult),
                      reads=[("x", kc), "vecs", ("rs", 1)], writes=[("x", kc)])
        for q in range(4):
            S.add("sync", lambda e, q=q: e.dma_start(out=ov[:, 4 * q:4 * q + 4, :], in_=x_sb[:, 4 * q:4 * q + 4, 2:1026]),
                  reads=[("x", kc) for kc in range(4 * q, 4 * q + 4)], dma_tag="out%d" % q, final=True)
        S.emit()
    return nc

H_RET = 6
def ret_consts(h):
    pos = np.arange(8192, dtype=np.float32)
    half = 64
    inv_freq = (np.float32(10000.0) ** (-np.arange(half, dtype=np.float32) / np.float32(half))).astype(np.float32)
    ang = (pos[:, None] * inv_freq[None, :]).astype(np.float32)
    cos = np.cos(ang).astype(np.float32).T
    sin = np.sin(ang).astype(np.float32).T
    CT = np.concatenate([cos, cos], axis=0)
    ST = np.concatenate([-sin, sin], axis=0)
    log_g = np.log1p(-np.exp2(np.float32(-5.0 - h))).astype(np.float32)
    idx = np.arange(128, dtype=np.float32)
    dq = np.exp((idx + 1.0) * log_g).astype(np.float32)
    dk = (np.exp(-(idx + 1.0) * log_g) * np.float32(128 ** -0.5)).astype(np.float32)
    gC = np.exp(np.float32(128.0) * log_g).astype(np.float32)
    return np.stack([CT, ST]).astype(np.float32), dq, dk, gC
def prep_B(proj, d, l, c):
    hr = c % 6; ha = c % 6; gg = c // 2; hh = c % 2
    sl = lambda base, h: slice(base + h * 128, base + (h + 1) * 128)
    q = proj[:, sl(0, hr)]; k = proj[:, sl(768, hr)]; v = proj[:, sl(1536, hr)]; g = proj[:, sl(2304, hr)]
    sw = lambda a: np.concatenate([a[:, 64:], a[:, :64]], axis=1)
    r_qk = np.ascontiguousarray(np.stack([q.T, sw(q).T, k.T, sw(k).T]))
    r_cs, dq, dk, gC = ret_consts(hr)
    tab = np.zeros((128, 1152), np.float32)
    tab[:, 0:512] = np.tile(dq, 4)[None, :]
    tab[:, 512:1024] = np.tile(dk, 4)[None, :]
    tab[:, 1024:1152] = d["ret_norm_w"][l][hr * 128:(hr + 1) * 128][None, :]
    sc = np.zeros((128, 4), np.float32); sc[:, 0] = gC
    kk = np.arange(128)
    cmask = (kk[None, :] >= kk[:, None]).astype(np.float32)
    aq = proj[:, sl(3072, ha)]; ak = proj[:, sl(3840, ha)]; av = proj[:, sl(4608, ha)]
    def dil(a, dd):
        return np.ascontiguousarray(a.reshape(8192 // dd, dd, 128).transpose(1, 0, 2).reshape(8192, 128).T)
    a_q = np.stack([dil(aq, dd) for dd in (1, 4, 16)]); a_k = np.stack([dil(ak, dd) for dd in (1, 4, 16)])
    a = np.arange(128)[:, None]; cc = np.arange(256)[None, :]
    dist = a + 128 - cc
    band = (dist >= 0) & (dist <= 128)
    m_mid = np.where(band, 0.0, -1e30).astype(np.float32)
    m_first = np.where(band & (cc >= 128), 0.0, -1e30).astype(np.float32)
    t0 = hh * 4096
    gu = proj[t0:t0 + 4096, 5376 + gg * 128:5376 + (gg + 1) * 128]
    gv = proj[t0:t0 + 4096, 5888 + gg * 128:5888 + (gg + 1) * 128]
    f = np.ascontiguousarray
    return {"r_qk": r_qk, "r_cs": r_cs, "r_v": f(v), "r_g": f(g), "r_tab": tab, "r_sc": sc, "cmask": cmask,
            "ident": np.eye(128, dtype=np.float32), "a_q": f(a_q), "a_k": f(a_k), "a_v": f(av),
            "a_mask": np.stack([m_mid, m_first]), "gm_u": f(gu), "gm_v": f(gv),
            "gm_wT": f(d["gmlp_ws"][l][gg].T), "gm_bs": f(d["gmlp_bs"][l][gg][:, None])}


_PROGS = {}


def _prog(name):
    if name not in _PROGS:
        if name == "A":
            _PROGS[name] = build_A()
        elif name == "B":
            _PROGS[name] = build_B()
        elif name == "C":
            _PROGS[name] = build_C(final=False)
        else:
            _PROGS[name] = build_C(final=True)
    return _PROGS[name]


def _prep_C_common(d, l):
    vecs = np.zeros((128, 64), np.float32)
    mixw = np.concatenate([np.ones(768, np.float32), d["att_norm_w"][l], d["gmlp_out_w"][l]])
    vecs[:, 0:16] = mixw.reshape(16, 128).T
    vecs[:, 16:32] = d["norm2_w"][l].reshape(16, 128).T
    vecs[:, 32:48] = d["final_norm_w"].reshape(16, 128).T
    cw = np.zeros((128, 4, 88), np.float32)
    for j in range(3):
        cw[:, j, :] = d["conv_w"][l][j].reshape(88, 128).T
    cw[:, 3, :] = d["conv_b"][l].reshape(88, 128).T
    return vecs, cw


def _halo(a, c):
    out = np.zeros((1026, a.shape[1]), np.float32)
    out[2:] = a[c * 1024:(c + 1) * 1024]
    if c > 0:
        out[:2] = a[c * 1024 - 2:c * 1024]
    return np.ascontiguousarray(out.T)


def kernel(**inputs):
    d = {k: np.asarray(v) for k, v in inputs.items()}
    x = np.ascontiguousarray(d["x"][0]).astype(np.float32)
    cores = list(range(8))
    depth = d["w_in"].shape[0]
    for l in range(depth):
        n1w = np.ascontiguousarray(d["norm1_w"][l].reshape(16, 128).T)
        lnw = np.ascontiguousarray(d["gmlp_ln_w"][l].reshape(4, 128).T)
        w_in = np.ascontiguousarray(d["w_in"][l])
        in_maps = [{"xT": np.ascontiguousarray(x[c * 1024:(c + 1) * 1024].T), "n1w": n1w, "lnw": lnw, "w_in": w_in} for c in cores]
        res = run_bass_kernel_spmd(_prog("A"), in_maps, core_ids=cores)
        proj = np.concatenate([r["projT"].T for r in res.results], axis=0)
        del in_maps, res
        in_maps = [prep_B(proj, d, l, c) for c in cores]
        res = run_bass_kernel_spmd(_prog("B"), in_maps, core_ids=cores)
        mixed = np.empty((8192, 2048), np.float32)
        for c in range(6):
            mixed[:, c * 128:(c + 1) * 128] = res.results[c]["ret_o"]
            mixed[:, 768 + c * 128:768 + (c + 1) * 128] = res.results[c]["att_o"]
        for c in range(8):
            gg, hh = c // 2, c % 2
            mixed[hh * 4096:(hh + 1) * 4096, 1536 + gg * 128:1536 + (gg + 1) * 128] = res.results[c]["gm_o"]
        del in_maps, res, proj
        vecs, cw = _prep_C_common(d, l)
        w_out = np.ascontiguousarray(d["w_out"][l])
        w_up = np.ascontiguousarray(d["w_up"][l])
        w_down = np.ascontiguousarray(d["w_down"][l])
        in_maps = [{"xT": _halo(x, c), "mT": _halo(mixed, c), "vecs": vecs, "cw": cw, "w_out": w_out, "w_up": w_up, "w_down": w_down} for c in cores]
        res = run_bass_kernel_spmd(_prog("C" if l < depth - 1 else "CF"), in_maps, core_ids=cores)
        x = np.concatenate([r["outT"].T for r in res.results], axis=0).astype(np.float32)
        del in_maps, res, mixed
    return np.ascontiguousarray(x[None]).astype(np.float32)
```

```python
import contextlib
import numpy as np
from concourse.bass_utils import run_bass_kernel_spmd
import concourse.bass as bass
import concourse.mybir as mybir

F32 = mybir.dt.float32
BF16 = mybir.dt.bfloat16
AF = mybir.ActivationFunctionType
ALU = mybir.AluOpType
AX = mybir.AxisListType

ENGS = ["sync", "gpsimd", "scalar", "vector", "tensor"]


class Op:
    __slots__ = ("idx", "eng", "fn", "deps", "dma_tag", "dma_val", "sig", "has_dep")

    def __init__(self, idx, eng, fn, dma_tag):
        self.idx = idx
        self.eng = eng
        self.fn = fn
        self.deps = set()
        self.dma_tag = dma_tag
        self.dma_val = 0
        self.sig = 0
        self.has_dep = False


class Sched:
    def __init__(self, nc, serialize=False):
        self.nc = nc
        self.serialize = serialize
        self.last_compute = None
        self.ops = []
        self.last_w = {}
        self.readers = {}
        self.dma_cnt = {}
        self.final_dma = []

    def add(self, eng, fn, reads=(), writes=(), dma_tag=None, final=False):
        op = Op(len(self.ops), eng, fn, dma_tag)
        deps = set()
        for r in reads:
            if r in self.last_w:
                deps.add(self.last_w[r])
        for w in writes:
            if w in self.last_w:
                deps.add(self.last_w[w])
            deps |= self.readers.get(w, set())
        for r in reads:
            self.readers.setdefault(r, set()).add(op)
        for w in writes:
            self.last_w[w] = op
            self.readers[w] = set()
        if self.serialize and dma_tag is None:
            if self.last_compute is not None:
                deps.add(self.last_compute)
            self.last_compute = op
        deps.discard(op)
        op.deps = deps
        for d in deps:
            d.has_dep = True
        if dma_tag is not None:
            self.dma_cnt[dma_tag] = self.dma_cnt.get(dma_tag, 0) + 16
            op.dma_val = self.dma_cnt[dma_tag]
        if final:
            op.has_dep = True
            self.final_dma.append(op)
        self.ops.append(op)
        return op

    def emit(self):
        nc = self.nc
        cnt = {e: 0 for e in ENGS}
        for op in self.ops:
            if op.dma_tag is None and op.has_dep:
                cnt[op.eng] += 1
                op.sig = cnt[op.eng]
        tags = sorted(self.dma_cnt.keys())
        import contextlib
        with contextlib.ExitStack() as st:
            csem = {e: st.enter_context(nc.semaphore("cs_" + e)) for e in ENGS}
            dsem = {t: st.enter_context(nc.semaphore("ds_%d" % i)) for i, t in enumerate(tags)}
            block = st.enter_context(nc.Block())
            ops = self.ops
            final_dma = self.final_dma

            def run(engname, e):
                waited_c = {x: 0 for x in ENGS}
                waited_d = {}
                for op in ops:
                    if op.eng != engname:
                        continue
                    need_c = {}
                    need_d = {}
                    for d in op.deps:
                        if d.dma_tag is not None:
                            need_d[d.dma_tag] = max(need_d.get(d.dma_tag, 0), d.dma_val)
                        else:
                            if d.eng == engname and engname == "tensor":
                                continue
                            need_c[d.eng] = max(need_c.get(d.eng, 0), d.sig)
                    for pe, v in need_c.items():
                        if v > waited_c[pe]:
                            e.wait_ge(csem[pe], v)
                            waited_c[pe] = v
                    for t, v in need_d.items():
                        if v > waited_d.get(t, 0):
                            e.wait_ge(dsem[t], v)
                            waited_d[t] = v
                    ins = op.fn(e)
                    if op.dma_tag is not None:
                        ins.then_inc(dsem[op.dma_tag], 16)
                    elif op.has_dep:
                        ins.then_inc(csem[engname], 1)
                if engname == "sync":
                    need_d = {}
                    for d in final_dma:
                        need_d[d.dma_tag] = max(need_d.get(d.dma_tag, 0), d.dma_val)
                    for t, v in need_d.items():
                        e.wait_ge(dsem[t], v)

            @block.sync
            def _(e):
                run("sync", e)

            @block.gpsimd
            def _(e):
                run("gpsimd", e)

            @block.scalar
            def _(e):
                run("scalar", e)

            @block.vector
            def _(e):
                run("vector", e)

            @block.tensor
            def _(e):
                run("tensor", e)


T = 1024
DM = 2048
INW = 6400
EPS = 1e-6
GC = 0.7978845608028654


def gelu_ops(S, nm, src_key, src_ap, dst_key, dst_ap, tmp_aps, tmp_keys):
    u, t = tmp_aps
    uk, tk = tmp_keys
    S.add("scalar", lambda e: e.activation(out=u, in_=src_ap, func=AF.Square), reads=[src_key], writes=[uk])
    S.add("vector", lambda e: e.tensor_scalar(out=u, in0=u, scalar1=0.044715, scalar2=1.0, op0=ALU.mult, op1=ALU.add),
          reads=[uk], writes=[uk])
    S.add("vector", lambda e: e.tensor_tensor(out=t, in0=u, in1=src_ap, op=ALU.mult), reads=[uk, src_key], writes=[tk])
    S.add("scalar", lambda e: e.activation(out=t, in_=t, func=AF.Sigmoid, scale=2.0 * GC), reads=[tk], writes=[tk])
    S.add("vector", lambda e: e.tensor_tensor(out=dst_ap, in0=t, in1=src_ap, op=ALU.mult), reads=[tk, src_key], writes=[dst_key])


def build_A():
    nc = bass.Bass("TRN2", target_bir_lowering=False)
    xT = nc.dram_tensor("xT", [DM, T], F32, kind="ExternalInput").ap()
    n1w = nc.dram_tensor("n1w", [128, 16], F32, kind="ExternalInput").ap()
    lnw = nc.dram_tensor("lnw", [128, 4], F32, kind="ExternalInput").ap()
    w_in = nc.dram_tensor("w_in", [DM, INW], F32, kind="ExternalInput").ap()
    projT = nc.dram_tensor("projT", [INW, T], F32, kind="ExternalOutput").ap()
    xv = xT.rearrange("(kc p) t -> p kc t", p=128)
    wv = w_in.rearrange("(kc p) n -> p kc n", p=128)
    WB = 640
    NWB = INW // WB
    with contextlib.ExitStack() as st:
        def sb(name, shape, dt):
            return st.enter_context(nc.sbuf_tensor(name, shape, dt))
        x_sb = sb("x_sb", [128, 16, T], F32)
        hT = sb("hT", [128, 16, T], BF16)
        sq = sb("sq", [128, 2, 512], BF16)
        ones = sb("ones", [128, 128], BF16)
        n1 = sb("n1", [128, 16], F32)
        ln = sb("ln", [128, 4], F32)
        rstd = sb("rstd", [128, T], F32)
        wb = [sb("wb%d" % i, [128, 16, WB], BF16) for i in range(3)]
        stg = [sb("stg%d" % i, [128, T], F32) for i in range(3)]
        gvb = sb("gvb", [128, 4, T], BF16)
        gsq = sb("gsq", [128, 4, T], BF16)
        tu = sb("tu", [128, 512], F32)
        tt = sb("tt", [128, 512], F32)
        mu = sb("mu", [128, T], F32)
        lrs = sb("lrs", [128, T], F32)
        epsb = sb("epsb", [128, 1], F32)
        gv = x_sb
        ps = [st.enter_context(nc.psum_tensor("ps%d" % i, [128, 512], F32)) for i in range(8)]
        S = Sched(nc)
        S.add("vector", lambda e: e.memset(ones[:], 1.0), writes=["ones"])
        S.add("vector", lambda e: e.memset(epsb[:], EPS), writes=["epsb"])
        S.add("sync", lambda e: e.dma_start(out=n1[:], in_=n1w), writes=["n1"], dma_tag="n1")
        S.add("sync", lambda e: e.dma_start(out=ln[:], in_=lnw), writes=["ln"], dma_tag="ln")
        for q in range(4):
            S.add("sync", lambda e, q=q: e.dma_start(out=x_sb[:, 4 * q:4 * q + 4, :], in_=xv[:, 4 * q:4 * q + 4, :]),
                  writes=[("x", kc) for kc in range(4 * q, 4 * q + 4)], dma_tag="x%d" % q)
        def load_w(j):
            s = j % 3
            for h in range(2):
                S.add("gpsimd", lambda e, j=j, s=s, h=h: e.dma_start(out=wb[s][:, 8 * h:8 * h + 8, :], in_=wv[:, 8 * h:8 * h + 8, j * WB:(j + 1) * WB]),
                      writes=[("w", s, h)], dma_tag="w%d" % s)
        load_w(0)
        load_w(1)
        for tb in range(2):
            tsl = slice(tb * 512, (tb + 1) * 512)
            for kc in range(16):
                b = kc % 2
                S.add("scalar", lambda e, kc=kc, b=b, tsl=tsl: e.activation(out=sq[:, b, :], in_=x_sb[:, kc, tsl], func=AF.Square),
                      reads=[("x", kc)], writes=[("sq", b)])
                S.add("tensor", lambda e, kc=kc, b=b, tb=tb: e.matmul(ps[6 + tb][:], lhsT=ones[:], rhs=sq[:, b, :], start=(kc == 0), stop=(kc == 15)),
                      reads=["ones", ("sq", b)], writes=[("ps", 6 + tb)])
            S.add("scalar", lambda e, tb=tb, tsl=tsl: e.activation(out=rstd[:, tsl], in_=ps[6 + tb][:], func=AF.Sqrt, scale=1.0 / DM, bias=epsb[:, 0:1]),
                  reads=[("ps", 6 + tb), "epsb"], writes=[("rstd", tb)])
            S.add("vector", lambda e, tsl=tsl: e.reciprocal(out=rstd[:, tsl], in_=rstd[:, tsl]), reads=[("rstd", tb)], writes=[("rstd", tb)])
            for kc in range(16):
                S.add("vector", lambda e, kc=kc, tsl=tsl: e.scalar_tensor_tensor(out=hT[:, kc, tsl], in0=x_sb[:, kc, tsl], scalar=n1[:, kc:kc + 1],
                                                                               in1=rstd[:, tsl], op0=ALU.mult, op1=ALU.mult),
                      reads=[("x", kc), "n1", ("rstd", tb)], writes=[("hT", kc, tb)])
        pi = 0
        si = 0
        for j in range(NWB):
            s = j % 3
            if j + 2 < NWB:
                load_w(j + 2)
            for mm in range(WB // 128):
                m = j * (WB // 128) + mm
                sg = si % 3
                si += 1
                for tb in range(2):
                    tsl = slice(tb * 512, (tb + 1) * 512)
                    p = pi % 6
                    pi += 1
                    for kc in range(16):
                        S.add("tensor", lambda e, p=p, s=s, kc=kc, mm=mm, tsl=tsl: e.matmul(ps[p][:], lhsT=wb[s][:, kc, mm * 128:(mm + 1) * 128],
                                                                                         rhs=hT[:, kc, tsl], start=(kc == 0), stop=(kc == 15)),
                              reads=[("w", s, kc // 8), ("hT", kc, tb)], writes=[("ps", p)])
                    if m < 42:
                        if tb == 0:
                            S.add("scalar", lambda e, p=p, sg=sg, tsl=tsl: e.copy(out=stg[sg][:, tsl], in_=ps[p][:]),
                                  reads=[("ps", p)], writes=[("stg", sg, tb)])
                        else:
                            S.add("vector", lambda e, p=p, sg=sg, tsl=tsl: e.tensor_copy(out=stg[sg][:, tsl], in_=ps[p][:]),
                                  reads=[("ps", p)], writes=[("stg", sg, tb)])
                    elif m < 46:
                        gelu_ops(S, "gu", ("ps", p), ps[p][:], ("stg", sg, tb), stg[sg][:, tsl], (tu[:], tt[:]), ("tu", "tt"))
                    else:
                        c = m - 46
                        gelu_ops(S, "gv", ("ps", p), ps[p][:], ("x", c), gv[:, c, tsl], (tu[:], tt[:]), ("tu", "tt"))
                        S.add("vector", lambda e, c=c, tsl=tsl: e.tensor_copy(out=gvb[:, c, tsl], in_=gv[:, c, tsl]),
                              reads=[("x", c)], writes=[("gvb", c, tb)])
                        S.add("scalar", lambda e, c=c, tsl=tsl: e.activation(out=gsq[:, c, tsl], in_=gv[:, c, tsl], func=AF.Square),
                              reads=[("x", c)], writes=[("gsq", c, tb)])
                if m < 46:
                    S.add("sync", lambda e, m=m, sg=sg: e.dma_start(out=projT[m * 128:(m + 1) * 128, :], in_=stg[sg][:]),
                          reads=[("stg", sg, 0), ("stg", sg, 1)], dma_tag="st%d" % sg, final=True)
        for tb in range(2):
            tsl = slice(tb * 512, (tb + 1) * 512)
            for c in range(4):
                S.add("tensor", lambda e, c=c, tsl=tsl: e.matmul(ps[0][:], lhsT=ones[:], rhs=gvb[:, c, tsl], start=(c == 0), stop=(c == 3)),
                      reads=["ones", ("gvb", c, tb)], writes=[("ps", 0)])
            for c in range(4):
                S.add("tensor", lambda e, c=c, tsl=tsl: e.matmul(ps[1][:], lhsT=ones[:], rhs=gsq[:, c, tsl], start=(c == 0), stop=(c == 3)),
                      reads=["ones", ("gsq", c, tb)], writes=[("ps", 1)])
            S.add("scalar", lambda e, tsl=tsl: e.mul(out=mu[:, tsl], in_=ps[0][:], mul=1.0 / 512), reads=[("ps", 0)], writes=[("mu", tb)])
            S.add("vector", lambda e, tsl=tsl: e.tensor_tensor(out=tu[:], in0=mu[:, tsl], in1=mu[:, tsl], op=ALU.mult), reads=[("mu", tb)], writes=["tu"])
            S.add("vector", lambda e, tsl=tsl: e.scalar_tensor_tensor(out=tt[:], in0=ps[1][:], scalar=1.0 / 512, in1=tu[:], op0=ALU.mult, op1=ALU.subtract),
                  reads=[("ps", 1), "tu"], writes=["tt"])
            S.add("scalar", lambda e, tsl=tsl: e.activation(out=lrs[:, tsl], in_=tt[:], func=AF.Sqrt, bias=epsb[:, 0:1]), reads=["tt", "epsb"], writes=[("lrs", tb)])
            S.add("vector", lambda e, tsl=tsl: e.reciprocal(out=lrs[:, tsl], in_=lrs[:, tsl]), reads=[("lrs", tb)], writes=[("lrs", tb)])
            for c in range(4):
                sg = c % 3
                S.add("vector", lambda e, c=c, tsl=tsl: e.tensor_tensor(out=gv[:, c, tsl], in0=gv[:, c, tsl], in1=mu[:, tsl], op=ALU.subtract),
                      reads=[("x", c), ("mu", tb)], writes=[("x", c)])
                S.add("vector", lambda e, c=c, tsl=tsl: e.scalar_tensor_tensor(out=gv[:, c, tsl], in0=gv[:, c, tsl], scalar=ln[:, c:c + 1], in1=lrs[:, tsl],
                                                                            op0=ALU.mult, op1=ALU.mult),
                      reads=[("x", c), "ln", ("lrs", tb)], writes=[("x", c)])
        for c in range(4):
            m = 46 + c
            S.add("sync", lambda e, m=m, c=c: e.dma_start(out=projT[m * 128:(m + 1) * 128, :], in_=gv[:, c, :]),
                  reads=[("x", c)], dma_tag="stgv", final=True)
        S.emit()
    return nc


SEQ = 8192
EPS = 1e-6
SCL = 128 ** -0.5
DILS = (1, 4, 16)


def build_B():
    nc = bass.Bass("TRN2", target_bir_lowering=False)
    def din(name, shape):
        return nc.dram_tensor(name, shape, F32, kind="ExternalInput").ap()
    r_qk = din("r_qk", [4, 128, SEQ])
    r_cs = din("r_cs", [2, 128, SEQ])
    r_v = din("r_v", [SEQ, 128])
    r_g = din("r_g", [SEQ, 128])
    r_tab = din("r_tab", [128, 1152])
    r_sc = din("r_sc", [128, 4])
    cmask = din("cmask", [128, 128])
    ident_d = din("ident", [128, 128])
    a_q = din("a_q", [3, 128, SEQ])
    a_k = din("a_k", [3, 128, SEQ])
    a_v = din("a_v", [SEQ, 128])
    a_mask = din("a_mask", [2, 128, 256])
    gm_u = din("gm_u", [4096, 128])
    gm_v = din("gm_v", [4096, 128])
    gm_wT = din("gm_wT", [128, 128])
    gm_bs = din("gm_bs", [128, 1])
    ret_o = nc.dram_tensor("ret_o", [SEQ, 128], F32, kind="ExternalOutput").ap()
    att_o = nc.dram_tensor("att_o", [SEQ, 128], F32, kind="ExternalOutput").ap()
    gm_o = nc.dram_tensor("gm_o", [4096, 128], F32, kind="ExternalOutput").ap()
    scr = nc.dram_tensor("scr", [3, SEQ, 130], F32).ap()
    with contextlib.ExitStack() as st:
        def sb(name, shape, dt):
            return st.enter_context(nc.sbuf_tensor(name, shape, dt))
        qt = sb("qt", [128, SEQ], BF16)
        kt = sb("kt", [128, SEQ], BF16)
        vb = sb("vb", [128, 64, 128], BF16)
        rin = [sb("rin%d" % i, [128, 6, 512], F32) for i in range(2)]
        rt1 = sb("rt1", [128, 512], F32)
        rt2 = sb("rt2", [128, 512], F32)
        tab = sb("tab", [128, 1152], F32)
        sc = sb("sc", [128, 4], F32)
        cm = sb("cm", [128, 128], F32)
        idf = sb("idf", [128, 128], F32)
        idb = sb("idb", [128, 128], BF16)
        gbuf = [sb("gbuf%d" % i, [128, 8, 128], F32) for i in range(2)]
        obuf = [sb("obuf%d" % i, [128, 8, 128], F32) for i in range(2)]
        ktok = [sb("ktok%d" % i, [128, 128], BF16) for i in range(2)]
        smb = [sb("smb%d" % i, [128, 128], BF16) for i in range(2)]
        Tst = sb("Tst", [128, 128], F32)
        Sbf = sb("Sbf", [128, 128], BF16)
        junk = sb("junk", [128, 256], F32)
        ssq = [sb("ssq%d" % i, [128, 1], F32) for i in range(2)]
        epsb = sb("epsb", [128, 1], F32)
        yb = sb("yb", [128, 128], F32)
        sgb = sb("sgb", [128, 128], F32)
        am = sb("am", [128, 2, 256], F32)
        s_sb = [sb("s_sb%d" % i, [128, 256], F32) for i in range(2)]
        pb = [sb("pb%d" % i, [128, 256], BF16) for i in range(2)]
        ptb = [sb("ptb%d" % i, [128, 2, 128], BF16) for i in range(2)]
        mx = [sb("mx%d" % i, [128, 2], F32) for i in range(2)]
        nst = [sb("nst%d" % i, [128, 130], F32) for i in range(3)]
        mrg = [sb("mrg%d" % i, [128, 3, 130], F32) for i in range(2)]
        mw = [sb("mw%d" % i, [128, 8], F32) for i in range(2)]
        gu_sb = sb("gu_sb", [128, 32, 128], F32)
        gv_sb = sb("gv_sb", [128, 32, 128], BF16)
        gwf = sb("gwf", [128, 128], F32)
        gwb = sb("gwb", [128, 128], BF16)
        gbs = sb("gbs", [128, 1], F32)
        ps = [st.enter_context(nc.psum_tensor("ps%d" % i, [128, 512], F32)) for i in range(6)]
        psb = [st.enter_context(nc.psum_tensor("psb%d" % i, [128, 2, 128], BF16)) for i in range(2)]
        S = Sched(nc)
        pc = [0]

        def nps():
            p = pc[0] % 6
            pc[0] += 1
            return p
        S.add("sync", lambda e: e.dma_start(out=tab[:], in_=r_tab), writes=["tab"], dma_tag="c0")
        S.add("sync", lambda e: e.dma_start(out=sc[:], in_=r_sc), writes=["sc"], dma_tag="c1")
        S.add("sync", lambda e: e.dma_start(out=cm[:], in_=cmask), writes=["cm"], dma_tag="c2")
        S.add("sync", lambda e: e.dma_start(out=idf[:], in_=ident_d), writes=["idf"], dma_tag="c3")
        S.add("sync", lambda e: e.dma_start(out=am[:], in_=a_mask.rearrange("m p k -> p m k")), writes=["am"], dma_tag="c4")
        S.add("sync", lambda e: e.dma_start(out=gwf[:], in_=gm_wT), writes=["gwf"], dma_tag="c5")
        S.add("sync", lambda e: e.dma_start(out=gbs[:], in_=gm_bs), writes=["gbs"], dma_tag="c6")
        S.add("vector", lambda e: e.tensor_copy(out=idb[:], in_=idf[:]), reads=["idf"], writes=["idb"])
        S.add("vector", lambda e: e.memset(epsb[:], EPS), writes=["epsb"])
        S.add("vector", lambda e: e.memset(Tst[:], 0.0), writes=["T"])
        S.add("vector", lambda e: e.memset(Sbf[:], 0.0), writes=["Sbf"])
        S.add("gpsimd", lambda e: e.dma_start(out=vb[:], in_=r_v.rearrange("(n p) e -> p n e", p=128)), writes=[("vbk", i) for i in range(64)], dma_tag="vb")
        def rot_block(bi):
            b = bi % 2
            tsl = slice(bi * 512, (bi + 1) * 512)
            S.add("sync", lambda e, b=b, tsl=tsl: e.dma_start(out=rin[b][:, 0:4, :], in_=r_qk[:, :, tsl].rearrange("f p t -> p f t")),
                  writes=[("rin", b, 0)], dma_tag="rin%d_0" % b)
            S.add("sync", lambda e, b=b, tsl=tsl: e.dma_start(out=rin[b][:, 4:6, :], in_=r_cs[:, :, tsl].rearrange("f p t -> p f t")),
                  writes=[("rin", b, 1)], dma_tag="rin%d_1" % b)
            rk = [("rin", b, 0), ("rin", b, 1)]
            for (i0, dst, dkey, toff) in ((0, qt, "qt", 0), (2, kt, "kt", 512)):
                S.add("vector", lambda e, b=b, i0=i0: e.tensor_tensor(out=rt1[:], in0=rin[b][:, i0, :], in1=rin[b][:, 4, :], op=ALU.mult), reads=rk, writes=["rt1"])
                S.add("gpsimd", lambda e, b=b, i0=i0: e.tensor_tensor(out=rt2[:], in0=rin[b][:, i0 + 1, :], in1=rin[b][:, 5, :], op=ALU.mult), reads=rk, writes=["rt2"])
                S.add("vector", lambda e: e.tensor_tensor(out=rt1[:], in0=rt1[:], in1=rt2[:], op=ALU.add), reads=["rt1", "rt2"], writes=["rt1"])
                S.add("vector", lambda e, dst=dst, tsl=tsl, toff=toff: e.tensor_tensor(out=dst[:, tsl], in0=rt1[:], in1=tab[:, toff:toff + 512], op=ALU.mult),
                      reads=["rt1", "tab"], writes=[(dkey, bi)])
        rot_block(0)
        oi = 0
        for n in range(64):
            bi = n // 4
            if n % 4 == 0 and bi + 1 < 16:
                rot_block(bi + 1)
            csl = slice(n * 128, (n + 1) * 128)
            kb = n % 2
            grp = n // 8
            gb = grp % 2
            if n % 8 == 0:
                S.add("sync", lambda e, gb=gb, grp=grp: e.dma_start(out=gbuf[gb][:], in_=r_g[grp * 1024:(grp + 1) * 1024, :].rearrange("(n p) e -> p n e", p=128)),
                      writes=[("gbuf", gb)], dma_tag="gbuf%d" % gb)
            S.add("tensor", lambda e, kb=kb, csl=csl: e.transpose(out=psb[kb][:, 0, :], in_=kt[:, csl], identity=idb[:]), reads=[("kt", bi), "idb"], writes=[("psb", kb)])
            S.add("scalar", lambda e, kb=kb: e.copy(out=ktok[kb][:], in_=psb[kb][:, 0, :]), reads=[("psb", kb)], writes=[("ktok", kb)])
            p1 = nps()
            S.add("tensor", lambda e, p1=p1, csl=csl: e.matmul(ps[p1][:, 0:128], lhsT=kt[:, csl], rhs=qt[:, csl], start=True, stop=True),
                  reads=[("kt", bi), ("qt", bi)], writes=[("ps", p1)])
            S.add("vector", lambda e, p1=p1, kb=kb: e.tensor_tensor(out=smb[kb][:], in0=ps[p1][:, 0:128], in1=cm[:], op=ALU.mult), reads=[("ps", p1), "cm"], writes=[("smb", kb)])
            p2 = nps()
            S.add("tensor", lambda e, p2=p2, kb=kb, n=n: e.matmul(ps[p2][:, 0:128], lhsT=smb[kb][:], rhs=vb[:, n, :], start=True, stop=False),
                  reads=[("smb", kb), ("vbk", n)], writes=[("ps", p2)])
            S.add("tensor", lambda e, p2=p2, csl=csl: e.matmul(ps[p2][:, 0:128], lhsT=qt[:, csl], rhs=Sbf[:], start=False, stop=True),
                  reads=[("qt", bi), "Sbf"], writes=[("ps", p2)])
            p3 = nps()
            S.add("tensor", lambda e, p3=p3, kb=kb, n=n: e.matmul(ps[p3][:, 0:128], lhsT=ktok[kb][:], rhs=vb[:, n, :], start=True, stop=True),
                  reads=[("ktok", kb), ("vbk", n)], writes=[("ps", p3)])
            S.add("vector", lambda e, p3=p3: e.scalar_tensor_tensor(out=Tst[:], in0=Tst[:], scalar=sc[:, 0:1], in1=ps[p3][:, 0:128], op0=ALU.mult, op1=ALU.add),
                  reads=["T", "sc", ("ps", p3)], writes=["T"])
            S.add("scalar", lambda e: e.activation(out=Sbf[:], in_=Tst[:], func=AF.Identity, scale=sc[:, 0:1]), reads=["T", "sc"], writes=["Sbf"])
            sq_i = n % 2
            S.add("vector", lambda e, sq_i=sq_i: e.memset(ssq[sq_i][:], 0.0), writes=[("ssq", sq_i)])
            S.add("scalar", lambda e, p2=p2, sq_i=sq_i: e.activation(out=junk[:, 0:128], in_=ps[p2][:, 0:128], func=AF.Square, accum_out=ssq[sq_i][:]),
                  reads=[("ps", p2), ("ssq", sq_i)], writes=["junk", ("ssq", sq_i)])
            S.add("scalar", lambda e, sq_i=sq_i: e.activation(out=ssq[sq_i][:], in_=ssq[sq_i][:], func=AF.Sqrt, scale=1.0 / 128, bias=epsb[:, 0:1]),
                  reads=[("ssq", sq_i), "epsb"], writes=[("ssq", sq_i)])
            S.add("vector", lambda e, sq_i=sq_i: e.reciprocal(out=ssq[sq_i][:], in_=ssq[sq_i][:]), reads=[("ssq", sq_i)], writes=[("ssq", sq_i)])
            S.add("vector", lambda e, p2=p2, sq_i=sq_i: e.scalar_tensor_tensor(out=yb[:], in0=ps[p2][:, 0:128], scalar=ssq[sq_i][:, 0:1], in1=tab[:, 1024:1152], op0=ALU.mult, op1=ALU.mult),
                  reads=[("ps", p2), ("ssq", sq_i), "tab"], writes=["yb"])
            S.add("scalar", lambda e, gb=gb, n=n: e.activation(out=sgb[:], in_=gbuf[gb][:, n % 8, :], func=AF.Silu), reads=[("gbuf", gb)], writes=["sgb"])
            S.add("vector", lambda e, gb=gb, n=n: e.tensor_tensor(out=obuf[gb][:, n % 8, :], in0=yb[:], in1=sgb[:], op=ALU.mult), reads=["yb", "sgb"], writes=[("obuf", gb, n % 8)])
            if n % 8 == 7:
                S.add("sync", lambda e, gb=gb, grp=grp: e.dma_start(out=ret_o[grp * 1024:(grp + 1) * 1024, :].rearrange("(n p) e -> p n e", p=128), in_=obuf[gb][:]),
                      reads=[("obuf", gb, i) for i in range(8)], dma_tag="ro%d" % gb, final=True)
        bi_ctr = 0
        for pi_, d in enumerate(DILS):
            L = SEQ // d
            nb = L // 128
            for h2 in range(2):
                S.add("gpsimd", lambda e, pi_=pi_, h2=h2: e.dma_start(out=qt[:, h2 * 4096:(h2 + 1) * 4096], in_=a_q[pi_, :, h2 * 4096:(h2 + 1) * 4096]),
                      writes=[("qt", i) for i in range(8 * h2, 8 * h2 + 8)], dma_tag="aq%d" % h2)
                S.add("gpsimd", lambda e, pi_=pi_, h2=h2: e.dma_start(out=kt[:, h2 * 4096:(h2 + 1) * 4096], in_=a_k[pi_, :, h2 * 4096:(h2 + 1) * 4096]),
                      writes=[("kt", i) for i in range(8 * h2, 8 * h2 + 8)], dma_tag="ak%d" % h2)
            vview = a_v.rearrange("(b p d) e -> d p b e", p=128, d=d)
            for r in range(d):
                S.add("gpsimd", lambda e, r=r, nb=nb, vview=vview: e.dma_start(out=vb[:, r * nb:(r + 1) * nb, :], in_=vview[r]),
                      writes=[("vbk", i) for i in range(r * nb, (r + 1) * nb)], dma_tag="av%d" % r)
            for r in range(d):
                for b in range(nb):
                    B = r * nb + b
                    i2 = bi_ctr % 2
                    i3 = bi_ctr % 3
                    bi_ctr += 1
                    q0 = B * 128
                    k0 = (B - 1) * 128 if b > 0 else B * 128
                    mi = 0 if b > 0 else 1
                    hq = (q0 // 4096)
                    rd = [("qt", q0 // 512), ("kt", q0 // 512), ("kt", k0 // 512), ("kt", (k0 + 255) // 512)]
                    p1 = nps()
                    if b > 0:
                        S.add("tensor", lambda e, p1=p1, q0=q0, k0=k0: e.matmul(ps[p1][:, 0:256], lhsT=qt[:, q0:q0 + 128], rhs=kt[:, k0:k0 + 256], start=True, stop=True),
                              reads=rd, writes=[("ps", p1)])
                    else:
                        S.add("tensor", lambda e, p1=p1, q0=q0: e.matmul(ps[p1][:, 0:128], lhsT=qt[:, q0:q0 + 128], rhs=kt[:, q0:q0 + 128], start=True, stop=True),
                              reads=rd, writes=[("ps", p1)])
                        S.add("tensor", lambda e, p1=p1, q0=q0: e.matmul(ps[p1][:, 128:256], lhsT=qt[:, q0:q0 + 128], rhs=kt[:, q0:q0 + 128], start=True, stop=True),
                              reads=rd, writes=[("ps", p1)])
                    S.add("vector", lambda e, p1=p1, i2=i2, mi=mi: e.tensor_tensor(out=s_sb[i2][:], in0=ps[p1][:, 0:256], in1=am[:, mi, :], op=ALU.add),
                          reads=[("ps", p1), "am"], writes=[("s_sb", i2)])
                    S.add("vector", lambda e, i2=i2: e.reduce_max(out=mx[i2][:, 0:1], in_=s_sb[i2][:], axis=AX.X), reads=[("s_sb", i2)], writes=[("mx", i2)])
                    S.add("vector", lambda e, i2=i2: e.tensor_scalar(out=mx[i2][:, 1:2], in0=mx[i2][:, 0:1], scalar1=-SCL, scalar2=None, op0=ALU.mult),
                          reads=[("mx", i2)], writes=[("mx", i2)])
                    S.add("vector", lambda e, i3=i3: e.memset(nst[i3][:, 128:129], 0.0), writes=[("nst", i3, 1)])
                    S.add("scalar", lambda e, i2=i2, i3=i3: e.activation(out=pb[i2][:], in_=s_sb[i2][:], func=AF.Exp, scale=SCL, bias=mx[i2][:, 1:2], accum_out=nst[i3][:, 128:129]),
                          reads=[("s_sb", i2), ("mx", i2), ("nst", i3, 1)], writes=[("pb", i2), ("nst", i3, 1)])
                    S.add("vector", lambda e, i2=i2, i3=i3: e.tensor_scalar(out=nst[i3][:, 129:130], in0=mx[i2][:, 0:1], scalar1=SCL, scalar2=None, op0=ALU.mult),
                          reads=[("mx", i2)], writes=[("nst", i3, 2)])
                    for hh in range(2):
                        S.add("tensor", lambda e, i2=i2, hh=hh: e.transpose(out=psb[i2][:, hh, :], in_=pb[i2][:, hh * 128:(hh + 1) * 128], identity=idb[:]),
                              reads=[("pb", i2), "idb"], writes=[("psb", i2)])
                    S.add("scalar", lambda e, i2=i2: e.copy(out=ptb[i2][:], in_=psb[i2][:]), reads=[("psb", i2)], writes=[("ptb", i2)])
                    p2 = nps()
                    vprev = B - 1 if b > 0 else B
                    S.add("tensor", lambda e, p2=p2, i2=i2, vprev=vprev: e.matmul(ps[p2][:, 0:128], lhsT=ptb[i2][:, 0, :], rhs=vb[:, vprev, :], start=True, stop=False),
                          reads=[("ptb", i2), ("vbk", vprev), ("vbk", B)], writes=[("ps", p2)])
                    S.add("tensor", lambda e, p2=p2, i2=i2, B=B: e.matmul(ps[p2][:, 0:128], lhsT=ptb[i2][:, 1, :], rhs=vb[:, B, :], start=False, stop=True),
                          reads=[("ptb", i2), ("vbk", vprev), ("vbk", B)], writes=[("ps", p2)])
                    S.add("scalar", lambda e, p2=p2, i3=i3: e.copy(out=nst[i3][:, 0:128], in_=ps[p2][:, 0:128]), reads=[("ps", p2)], writes=[("nst", i3, 0)])
                    rows = scr[pi_].rearrange("(b p d) c -> d b p c", p=128, d=d)[r, b]
                    S.add("sync", lambda e, i3=i3, rows=rows: e.dma_start(out=rows, in_=nst[i3][:]),
                          reads=[("nst", i3, 0), ("nst", i3, 1), ("nst", i3, 2)], writes=[("scr", pi_, r, b)], dma_tag="nst%d" % i3)
        for nbk in range(64):
            i2 = nbk % 2
            keys = [("scr", 0, 0, nbk)] + [("scr", 1, r, nbk // 4) for r in range(4)] + [("scr", 2, r, nbk // 16) for r in range(16)]
            S.add("sync", lambda e, i2=i2, nbk=nbk: e.dma_start(out=mrg[i2][:], in_=scr[:, nbk * 128:(nbk + 1) * 128, :].rearrange("f p c -> p f c")),
                  reads=keys, writes=[("mrg", i2)], dma_tag="mrg%d" % i2)
            mk = [("mrg", i2)]
            w = mw[i2]
            S.add("vector", lambda e, i2=i2, w=w: e.tensor_tensor(out=w[:, 0:1], in0=mrg[i2][:, 0, 129:130], in1=mrg[i2][:, 1, 129:130], op=ALU.max), reads=mk, writes=[("mw", i2)])
            S.add("vector", lambda e, i2=i2, w=w: e.tensor_tensor(out=w[:, 0:1], in0=w[:, 0:1], in1=mrg[i2][:, 2, 129:130], op=ALU.max), reads=mk + [("mw", i2)], writes=[("mw", i2)])
            S.add("vector", lambda e, w=w: e.tensor_scalar(out=w[:, 1:2], in0=w[:, 0:1], scalar1=-1.0, scalar2=None, op0=ALU.mult), reads=[("mw", i2)], writes=[("mw", i2)])
            S.add("scalar", lambda e, i2=i2, w=w: e.activation(out=w[:, 2:5], in_=mrg[i2][:, :, 129], func=AF.Exp, bias=w[:, 1:2]), reads=mk + [("mw", i2)], writes=[("mw", i2)])
            S.add("vector", lambda e, i2=i2, w=w: e.tensor_tensor(out=w[:, 5:8], in0=w[:, 2:5], in1=mrg[i2][:, :, 128], op=ALU.mult), reads=mk + [("mw", i2)], writes=[("mw", i2)])
            S.add("vector", lambda e, w=w: e.reduce_sum(out=w[:, 0:1], in_=w[:, 5:8], axis=AX.X), reads=[("mw", i2)], writes=[("mw", i2)])
            S.add("vector", lambda e, w=w: e.reciprocal(out=w[:, 0:1], in_=w[:, 0:1]), reads=[("mw", i2)], writes=[("mw", i2)])
            S.add("vector", lambda e, w=w: e.tensor_scalar(out=w[:, 2:5], in0=w[:, 2:5], scalar1=w[:, 0:1], scalar2=None, op0=ALU.mult), reads=[("mw", i2)], writes=[("mw", i2)])
            ob = s_sb[i2]
            S.add("vector", lambda e, i2=i2, w=w, ob=ob: e.tensor_scalar(out=ob[:, 0:128], in0=mrg[i2][:, 0, 0:128], scalar1=w[:, 2:3], scalar2=None, op0=ALU.mult),
                  reads=mk + [("mw", i2)], writes=[("s_sb", i2)])
            for p_ in (1, 2):
                S.add("vector", lambda e, i2=i2, w=w, ob=ob, p_=p_: e.scalar_tensor_tensor(out=ob[:, 0:128], in0=mrg[i2][:, p_, 0:128], scalar=w[:, 2 + p_:3 + p_], in1=ob[:, 0:128],
                                                                                     op0=ALU.mult, op1=ALU.add),
                      reads=mk + [("mw", i2), ("s_sb", i2)], writes=[("s_sb", i2)])
            S.add("sync", lambda e, ob=ob, nbk=nbk: e.dma_start(out=att_o[nbk * 128:(nbk + 1) * 128, :], in_=ob[:, 0:128]), reads=[("s_sb", i2)], dma_tag="ao%d" % i2, final=True)
        S.add("vector", lambda e: e.tensor_tensor(out=gwb[:], in0=gwf[:], in1=cm[:], op=ALU.mult), reads=["gwf", "cm"], writes=["gwb"])
        S.add("sync", lambda e: e.dma_start(out=gu_sb[:], in_=gm_u.rearrange("(n p) e -> p n e", p=128)), writes=["gu"], dma_tag="gu")
        S.add("gpsimd", lambda e: e.dma_start(out=gv_sb[:], in_=gm_v.rearrange("(n p) e -> p n e", p=128)), writes=["gv"], dma_tag="gv")
        for n in range(32):
            p1 = nps()
            gb = (n // 8) % 2
            S.add("tensor", lambda e, p1=p1, n=n: e.matmul(ps[p1][:, 0:128], lhsT=gwb[:], rhs=gv_sb[:, n, :], start=True, stop=True), reads=["gwb", "gv"], writes=[("ps", p1)])
            S.add("vector", lambda e, p1=p1, n=n, gb=gb: e.scalar_tensor_tensor(out=obuf[gb][:, n % 8, :], in0=ps[p1][:, 0:128], scalar=gbs[:, 0:1], in1=gu_sb[:, n, :], op0=ALU.add, op1=ALU.mult),
                  reads=[("ps", p1), "gbs", "gu"], writes=[("obuf", gb, n % 8)])
            if n % 8 == 7:
                g8 = n // 8
                S.add("sync", lambda e, gb=gb, g8=g8: e.dma_start(out=gm_o[g8 * 1024:(g8 + 1) * 1024, :].rearrange("(n p) e -> p n e", p=128), in_=obuf[gb][:]),
                      reads=[("obuf", gb, i) for i in range(8)], dma_tag="go%d" % gb, final=True)
        S.emit()
    return nc


HALO = 32
TC = 1024 + HALO
DM = 2048
DFF = 5632
EPS = 1e-6
TBS = [(0, HALO), (HALO, HALO + 512), (HALO + 512, TC)]


def build_C(final=False, stage=None, serialize=False):
    nc = bass.Bass("TRN2", target_bir_lowering=False)
    xT_d = nc.dram_tensor("xT", [DM, TC], F32, kind="ExternalInput").ap()
    mT_d = nc.dram_tensor("mT", [DM, TC], F32, kind="ExternalInput").ap()
    vecs_d = nc.dram_tensor("vecs", [128, 64], F32, kind="ExternalInput").ap()
    cw_d = nc.dram_tensor("cw", [128, 4, 88], F32, kind="ExternalInput").ap()
    w_out = nc.dram_tensor("w_out", [DM, DM], F32, kind="ExternalInput").ap()
    w_up = nc.dram_tensor("w_up", [DM, 2 * DFF], F32, kind="ExternalInput").ap()
    w_down = nc.dram_tensor("w_down", [DFF, DM], F32, kind="ExternalInput").ap()
    outT = nc.dram_tensor("outT", [DM, 1024], F32, kind="ExternalOutput").ap()
    xv = xT_d.rearrange("(kc p) t -> p kc t", p=128)
    mv = mT_d.rearrange("(kc p) t -> p kc t", p=128)
    wov = w_out.rearrange("(kc p) n -> p kc n", p=128)
    wuv = w_up.rearrange("(kc p) n -> p kc n", p=128)
    wdv = w_down.rearrange("(kc p) n -> p kc n", p=128)
    ov = outT.rearrange("(kc p) t -> p kc t", p=128)
    with contextlib.ExitStack() as st:
        def sb(name, shape, dt):
            return st.enter_context(nc.sbuf_tensor(name, shape, dt))
        x_sb = sb("x_sb", [128, 16, TC], F32)
        nT = sb("nT", [128, 16, TC], BF16)
        act = sb("act", [128, 12, 1024], BF16)
        wb = [sb("wb%d" % i, [128, 16, 512], BF16) for i in range(3)]
        sq = [sb("sq%d" % i, [128, TC], BF16) for i in range(2)]
        rs = [sb("rs%d" % i, [128, TC], F32) for i in range(2)]
        hg = sb("hg", [128, TC], F32)
        hv = sb("hv", [128, TC], F32)
        ag = sb("ag", [128, 1024], F32)
        av = sb("av", [128, 1024], F32)
        mst = [hg, hv]
        ones = sb("ones", [128, 128], BF16)
        vecs = sb("vecs_sb", [128, 64], F32)
        cw = sb("cw_sb", [128, 4, 88], F32)
        epsb = sb("epsb", [128, 1], F32)
        ps = [st.enter_context(nc.psum_tensor("ps%d" % i, [128, 512], F32)) for i in range(8)]
        S = Sched(nc, serialize=serialize)
        pc = [0]

        def nps():
            p = pc[0] % 8
            pc[0] += 1
            return p
        wc = [0]

        def load_w(view, r0, nr, c0):
            s = wc[0] % 3
            wc[0] += 1
            h = (nr + 1) // 2
            for hi, (a, b) in enumerate(((0, h), (h, nr))):
                S.add("gpsimd", lambda e, s=s, a=a, b=b: e.dma_start(out=wb[s][:, a:b, :], in_=view[:, r0 + a:r0 + b, c0:c0 + 512]),
                      writes=[("w", s, hi)], dma_tag="w%d_%d" % (s, hi))
            return s, h

        S.add("vector", lambda e: e.memset(ones[:], 1.0), writes=["ones"])
        S.add("vector", lambda e: e.memset(epsb[:], EPS), writes=["epsb"])
        S.add("sync", lambda e: e.dma_start(out=vecs[:], in_=vecs_d), writes=["vecs"], dma_tag="vecs")
        S.add("sync", lambda e: e.dma_start(out=cw[:], in_=cw_d), writes=["cw"], dma_tag="cw")
        for q in range(4):
            for ti, (t0, t1) in enumerate(TBS):
                S.add("sync", lambda e, q=q, t0=t0, t1=t1: e.dma_start(out=x_sb[:, 4 * q:4 * q + 4, t0:t1], in_=xv[:, 4 * q:4 * q + 4, t0:t1]),
                      writes=[("x", kc, ti) for kc in range(4 * q, 4 * q + 4)], dma_tag="x%d_%d" % (q, ti))
        wq = []
        wq.append(load_w(wov, 0, 16, 0))
        wq.append(load_w(wov, 0, 16, 512))
        mi = [0]

        def load_m(c):
            b = mi[0] % 2
            mi[0] += 1
            for ti, (t0, t1) in enumerate(TBS):
                S.add("sync", lambda e, b=b, c=c, t0=t0, t1=t1: e.dma_start(out=mst[b][:, t0:t1], in_=mv[:, c, t0:t1]), writes=[("mst", b, ti)], dma_tag="mst%d_%d" % (b, ti))
            return b
        sqi = [0]

        def stats(src_list, rsl, n):
            pbs = [nps() for _ in range(3)]
            nsrc = len(src_list)
            for si, (src, skey) in enumerate(src_list):
                q = sqi[0] % 2
                sqi[0] += 1
                for ti, (t0, t1) in enumerate(TBS):
                    S.add("scalar", lambda e, q=q, src=src, t0=t0, t1=t1: e.activation(out=sq[q][:, t0:t1], in_=src()[:, t0:t1], func=AF.Square),
                          reads=[skey(ti)], writes=[("sq", q, ti)])
                    S.add("tensor", lambda e, q=q, t0=t0, t1=t1, si=si, p=pbs[ti]: e.matmul(ps[p][:, 0:t1 - t0], lhsT=ones[:], rhs=sq[q][:, t0:t1],
                                                                                         start=(si == 0), stop=(si == nsrc - 1)),
                          reads=["ones", ("sq", q, ti)], writes=[("ps", pbs[ti])])
            for ti, (t0, t1) in enumerate(TBS):
                S.add("scalar", lambda e, t0=t0, t1=t1, p=pbs[ti]: e.activation(out=rs[rsl][:, t0:t1], in_=ps[p][:, 0:t1 - t0], func=AF.Sqrt, scale=1.0 / n, bias=epsb[:, 0:1]),
                      reads=[("ps", pbs[ti]), "epsb"], writes=[("rs", rsl, ti)])
                S.add("vector", lambda e, t0=t0, t1=t1: e.reciprocal(out=rs[rsl][:, t0:t1], in_=rs[rsl][:, t0:t1]), reads=[("rs", rsl, ti)], writes=[("rs", rsl, ti)])
                S.add("vector", lambda e, t0=t0, t1=t1: e.tensor_scalar_min(out=rs[rsl][:, t0:t1], in0=rs[rsl][:, t0:t1], scalar1=1.0e4), reads=[("rs", rsl, ti)], writes=[("rs", rsl, ti)])

        for gi, (c0, c1) in enumerate(((6, 12), (12, 16))):
            pbs = [nps() for _ in range(3)]
            for c in range(c0, c1):
                b = load_m(c)
                q = sqi[0] % 2
                sqi[0] += 1
                for ti, (t0, t1) in enumerate(TBS):
                    S.add("scalar", lambda e, b=b, q=q, t0=t0, t1=t1: e.activation(out=sq[q][:, t0:t1], in_=mst[b][:, t0:t1], func=AF.Square),
                          reads=[("mst", b, ti)], writes=[("sq", q, ti)])
                    S.add("tensor", lambda e, q=q, t0=t0, t1=t1, c=c, c0=c0, c1=c1, p=pbs[ti]: e.matmul(ps[p][:, 0:t1 - t0], lhsT=ones[:], rhs=sq[q][:, t0:t1],
                                                                                                   start=(c == c0), stop=(c == c1 - 1)),
                          reads=["ones", ("sq", q, ti)], writes=[("ps", pbs[ti])])
            n = (c1 - c0) * 128
            for ti, (t0, t1) in enumerate(TBS):
                S.add("scalar", lambda e, gi=gi, t0=t0, t1=t1, p=pbs[ti], n=n: e.activation(out=rs[gi][:, t0:t1], in_=ps[p][:, 0:t1 - t0], func=AF.Sqrt, scale=1.0 / n, bias=epsb[:, 0:1]),
                      reads=[("ps", pbs[ti]), "epsb"], writes=[("rs", gi, ti)])
                S.add("vector", lambda e, gi=gi, t0=t0, t1=t1: e.reciprocal(out=rs[gi][:, t0:t1], in_=rs[gi][:, t0:t1]), reads=[("rs", gi, ti)], writes=[("rs", gi, ti)])
                S.add("vector", lambda e, gi=gi, t0=t0, t1=t1: e.tensor_scalar_min(out=rs[gi][:, t0:t1], in0=rs[gi][:, t0:t1], scalar1=1.0e4), reads=[("rs", gi, ti)], writes=[("rs", gi, ti)])
        for c in range(16):
            b = load_m(c)
            for ti, (t0, t1) in enumerate(TBS):
                if c < 6:
                    S.add("vector", lambda e, b=b, c=c, t0=t0, t1=t1: e.tensor_copy(out=nT[:, c, t0:t1], in_=mst[b][:, t0:t1]), reads=[("mst", b, ti)], writes=[("nT", c, ti)])
                else:
                    gi = 0 if c < 12 else 1
                    S.add("vector", lambda e, b=b, c=c, gi=gi, t0=t0, t1=t1: e.scalar_tensor_tensor(out=nT[:, c, t0:t1], in0=mst[b][:, t0:t1], scalar=vecs[:, c:c + 1], in1=rs[gi][:, t0:t1],
                                                                                                 op0=ALU.mult, op1=ALU.mult),
                          reads=[("mst", b, ti), "vecs", ("rs", gi, ti)], writes=[("nT", c, ti)])
        for jb in range(4):
            s, h = wq.pop(0)
            if jb + 2 < 4:
                wq.append(load_w(wov, 0, 16, (jb + 2) * 512))
            elif jb == 2:
                wq.append(load_w(wuv, 0, 16, 0))
            else:
                wq.append(load_w(wuv, 0, 16, DFF))
            for mm in range(4):
                mo = jb * 4 + mm
                for ti, (t0, t1) in enumerate(TBS):
                    p = nps()
                    for kc in range(16):
                        S.add("tensor", lambda e, p=p, s=s, kc=kc, mm=mm, t0=t0, t1=t1: e.matmul(ps[p][:, 0:t1 - t0], lhsT=wb[s][:, kc, mm * 128:(mm + 1) * 128],
                                                                                              rhs=nT[:, kc, t0:t1], start=(kc == 0), stop=(kc == 15)),
                              reads=[("w", s, 0 if kc < h else 1), ("nT", kc, ti)], writes=[("ps", p)])
                    S.add("vector", lambda e, p=p, mo=mo, t0=t0, t1=t1: e.tensor_tensor(out=x_sb[:, mo, t0:t1], in0=x_sb[:, mo, t0:t1], in1=ps[p][:, 0:t1 - t0], op=ALU.add),
                          reads=[("ps", p), ("x", mo, ti)], writes=[("x", mo, ti)])
        groups = [(0, 3), (3, 6), (6, 9), (9, 11)]
        if stage == 'wout':
            groups = []
            wq = []
        stats([((lambda kc=kc: x_sb[:, kc, :]), (lambda ti, kc=kc: ("x", kc, ti))) for kc in range(16)], 0, DM)
        for kc in range(16):
            for ti, (t0, t1) in enumerate(TBS):
                S.add("vector", lambda e, kc=kc, t0=t0, t1=t1: e.scalar_tensor_tensor(out=nT[:, kc, t0:t1], in0=x_sb[:, kc, t0:t1], scalar=vecs[:, 16 + kc:17 + kc], in1=rs[0][:, t0:t1],
                                                                                    op0=ALU.mult, op1=ALU.mult),
                      reads=[("x", kc, ti), "vecs", ("rs", 0, ti)], writes=[("nT", kc, ti)])
        for (b0, b1) in groups:
            for jb in range(b0, b1):
                sg, hgk = wq.pop(0)
                sv, hvk = wq.pop(0)
                for mm in range(4):
                    ch = jb * 4 + mm
                    al = (jb - b0) * 4 + mm
                    for (s, hsplit, hb, hkey) in ((sg, hgk, hg, "hg"), (sv, hvk, hv, "hv")):
                        for ti, (t0, t1) in enumerate(TBS):
                            p = nps()
                            for kc in range(16):
                                S.add("tensor", lambda e, p=p, s=s, kc=kc, mm=mm, t0=t0, t1=t1: e.matmul(ps[p][:, 0:t1 - t0], lhsT=wb[s][:, kc, mm * 128:(mm + 1) * 128],
                                                                                                      rhs=nT[:, kc, t0:t1], start=(kc == 0), stop=(kc == 15)),
                                      reads=[("w", s, 0 if kc < hsplit else 1), ("nT", kc, ti)], writes=[("ps", p)])
                            if ti == 1:
                                S.add("scalar", lambda e, p=p, hb=hb, t0=t0, t1=t1: e.copy(out=hb[:, t0:t1], in_=ps[p][:, 0:t1 - t0]), reads=[("ps", p)], writes=[(hkey, ti)])
                            else:
                                S.add("vector", lambda e, p=p, hb=hb, t0=t0, t1=t1: e.tensor_copy(out=hb[:, t0:t1], in_=ps[p][:, 0:t1 - t0]), reads=[("ps", p)], writes=[(hkey, ti)])
                    for o in range(2):
                        c0_ = HALO + o * 512
                        osl = slice(o * 512, (o + 1) * 512)
                        hk_all = lambda hkey: [(hkey, 0), (hkey, 1), (hkey, 2)]
                        for (hb, hkey, ab, akey, chx) in ((hg, "hg", ag, "ag", ch), (hv, "hv", av, "av", 44 + ch)):
                            S.add("scalar", lambda e, hb=hb, ab=ab, chx=chx, c0_=c0_, osl=osl: e.activation(out=ab[:, osl], in_=hb[:, c0_:c0_ + 512], func=AF.Identity,
                                                                                                         scale=cw[:, 2, chx:chx + 1], bias=cw[:, 3, chx:chx + 1]),
                                  reads=hk_all(hkey) + ["cw"], writes=[(akey, o)])
                            S.add("vector", lambda e, hb=hb, ab=ab, chx=chx, c0_=c0_, osl=osl: e.scalar_tensor_tensor(out=ab[:, osl], in0=hb[:, c0_ - 1:c0_ + 511], scalar=cw[:, 1, chx:chx + 1],
                                                                                                                   in1=ab[:, osl], op0=ALU.mult, op1=ALU.add),
                                  reads=hk_all(hkey) + ["cw", (akey, o)], writes=[(akey, o)])
                            S.add("vector", lambda e, hb=hb, ab=ab, chx=chx, c0_=c0_, osl=osl: e.scalar_tensor_tensor(out=ab[:, osl], in0=hb[:, c0_ - 2:c0_ + 510], scalar=cw[:, 0, chx:chx + 1],
                                                                                                                   in1=ab[:, osl], op0=ALU.mult, op1=ALU.add),
                                  reads=hk_all(hkey) + ["cw", (akey, o)], writes=[(akey, o)])
                        S.add("scalar", lambda e, osl=osl: e.activation(out=ag[:, osl], in_=ag[:, osl], func=AF.Silu), reads=[("ag", o)], writes=[("ag", o)])
                        S.add("vector", lambda e, al=al, osl=osl: e.tensor_tensor(out=act[:, al, osl], in0=ag[:, osl], in1=av[:, osl], op=ALU.mult),
                              reads=[("ag", o), ("av", o)], writes=[("act", al, o)])
                if jb + 1 < b1:
                    wq.append(load_w(wuv, 0, 16, (jb + 1) * 512))
                    wq.append(load_w(wuv, 0, 16, DFF + (jb + 1) * 512))
            nch = (b1 - b0) * 4
            r0 = b0 * 4
            wq.append(load_w(wdv, r0, nch, 0))
            wq.append(load_w(wdv, r0, nch, 512))
            for cb in range(4):
                s, h = wq.pop(0)
                if cb + 2 < 4:
                    wq.append(load_w(wdv, r0, nch, (cb + 2) * 512))
                elif b1 < 11:
                    if cb == 2:
                        wq.append(load_w(wuv, 0, 16, b1 * 512))
                    else:
                        wq.append(load_w(wuv, 0, 16, DFF + b1 * 512))
                for mm in range(4):
                    mo = cb * 4 + mm
                    for t2 in range(2):
                        p = nps()
                        for i in range(nch):
                            S.add("tensor", lambda e, p=p, s=s, i=i, mm=mm, t2=t2: e.matmul(ps[p][:], lhsT=wb[s][:, i, mm * 128:(mm + 1) * 128],
                                                                                         rhs=act[:, i, t2 * 512:(t2 + 1) * 512], start=(i == 0), stop=(i == nch - 1)),
                                  reads=[("w", s, 0 if i < h else 1), ("act", i, t2)], writes=[("ps", p)])
                        S.add("vector", lambda e, p=p, mo=mo, t2=t2: e.tensor_tensor(out=x_sb[:, mo, HALO + t2 * 512:HALO + (t2 + 1) * 512], in0=x_sb[:, mo, HALO + t2 * 512:HALO + (t2 + 1) * 512],
                                                                                    in1=ps[p][:], op=ALU.add),
                              reads=[("ps", p), ("x", mo, 1 + t2)], writes=[("x", mo, 1 + t2)])
        if final:
            stats([((lambda kc=kc: x_sb[:, kc, :]), (lambda ti, kc=kc: ("x", kc, ti))) for kc in range(16)], 1, DM)
            for kc in range(16):
                for ti, (t0, t1) in enumerate(TBS):
                    if ti == 0:
                        continue
                    S.add("vector", lambda e, kc=kc, t0=t0, t1=t1: e.scalar_tensor_tensor(out=x_sb[:, kc, t0:t1], in0=x_sb[:, kc, t0:t1], scalar=vecs[:, 32 + kc:33 + kc], in1=rs[1][:, t0:t1],
                                                                                        op0=ALU.mult, op1=ALU.mult),
                          reads=[("x", kc, ti), "vecs", ("rs", 1, ti)], writes=[("x", kc, ti)])
        for q in range(4):
            S.add("sync", lambda e, q=q: e.dma_start(out=ov[:, 4 * q:4 * q + 4, :], in_=x_sb[:, 4 * q:4 * q + 4, HALO:TC]),
                  reads=[("x", kc, ti) for kc in range(4 * q, 4 * q + 4) for ti in (1, 2)], dma_tag="out%d" % q, final=True)
        S.emit()
    return nc


H_RET = 6
def ret_consts(h):
    pos = np.arange(8192, dtype=np.float32)
    half = 64
    inv_freq = (np.float32(10000.0) ** (-np.arange(half, dtype=np.float32) / np.float32(half))).astype(np.float32)
    ang = (pos[:, None] * inv_freq[None, :]).astype(np.float32)
    cos = np.cos(ang).astype(np.float32).T
    sin = np.sin(ang).astype(np.float32).T
    CT = np.concatenate([cos, cos], axis=0)
    ST = np.concatenate([-sin, sin], axis=0)
    log_g = np.log1p(-np.exp2(np.float32(-5.0 - h))).astype(np.float32)
    idx = np.arange(128, dtype=np.float32)
    dq = np.exp((idx + 1.0) * log_g).astype(np.float32)
    dk = (np.exp(-(idx + 1.0) * log_g) * np.float32(128 ** -0.5)).astype(np.float32)
    gC = np.exp(np.float32(128.0) * log_g).astype(np.float32)
    return np.stack([CT, ST]).astype(np.float32), dq, dk, gC
def prep_B(proj, d, l, c):
    hr = c % 6; ha = c % 6; gg = c // 2; hh = c % 2
    sl = lambda base, h: slice(base + h * 128, base + (h + 1) * 128)
    q = proj[:, sl(0, hr)]; k = proj[:, sl(768, hr)]; v = proj[:, sl(1536, hr)]; g = proj[:, sl(2304, hr)]
    sw = lambda a: np.concatenate([a[:, 64:], a[:, :64]], axis=1)
    r_qk = np.ascontiguousarray(np.stack([q.T, sw(q).T, k.T, sw(k).T]))
    r_cs, dq, dk, gC = ret_consts(hr)
    tab = np.zeros((128, 1152), np.float32)
    tab[:, 0:512] = np.tile(dq, 4)[None, :]
    tab[:, 512:1024] = np.tile(dk, 4)[None, :]
    tab[:, 1024:1152] = d["ret_norm_w"][l][hr * 128:(hr + 1) * 128][None, :]
    sc = np.zeros((128, 4), np.float32); sc[:, 0] = gC
    kk = np.arange(128)
    cmask = (kk[None, :] >= kk[:, None]).astype(np.float32)
    aq = proj[:, sl(3072, ha)]; ak = proj[:, sl(3840, ha)]; av = proj[:, sl(4608, ha)]
    def dil(a, dd):
        return np.ascontiguousarray(a.reshape(8192 // dd, dd, 128).transpose(1, 0, 2).reshape(8192, 128).T)
    a_q = np.stack([dil(aq, dd) for dd in (1, 4, 16)]); a_k = np.stack([dil(ak, dd) for dd in (1, 4, 16)])
    a = np.arange(128)[:, None]; cc = np.arange(256)[None, :]
    dist = a + 128 - cc
    band = (dist >= 0) & (dist <= 128)
    m_mid = np.where(band, 0.0, -1e30).astype(np.float32)
    m_first = np.where(band & (cc >= 128), 0.0, -1e30).astype(np.float32)
    t0 = hh * 4096
    gu = proj[t0:t0 + 4096, 5376 + gg * 128:5376 + (gg + 1) * 128]
    gv = proj[t0:t0 + 4096, 5888 + gg * 128:5888 + (gg + 1) * 128]
    f = np.ascontiguousarray
    return {"r_qk": r_qk, "r_cs": r_cs, "r_v": f(v), "r_g": f(g), "r_tab": tab, "r_sc": sc, "cmask": cmask,
            "ident": np.eye(128, dtype=np.float32), "a_q": f(a_q), "a_k": f(a_k), "a_v": f(av),
            "a_mask": np.stack([m_mid, m_first]), "gm_u": f(gu), "gm_v": f(gv),
            "gm_wT": f(d["gmlp_ws"][l][gg].T), "gm_bs": f(d["gmlp_bs"][l][gg][:, None])}


_PROGS = {}


def _prog(name):
    if name not in _PROGS:
        if name == "A":
            _PROGS[name] = build_A()
        elif name == "B":
            _PROGS[name] = build_B()
        elif name == "C":
            _PROGS[name] = build_C(final=False)
        else:
            _PROGS[name] = build_C(final=True)
    return _PROGS[name]


def _prep_C_common(d, l):
    vecs = np.zeros((128, 64), np.float32)
    mixw = np.concatenate([np.ones(768, np.float32), d["att_norm_w"][l], d["gmlp_out_w"][l]])
    vecs[:, 0:16] = mixw.reshape(16, 128).T
    vecs[:, 16:32] = d["norm2_w"][l].reshape(16, 128).T
    vecs[:, 32:48] = d["final_norm_w"].reshape(16, 128).T
    cw = np.zeros((128, 4, 88), np.float32)
    for j in range(3):
        cw[:, j, :] = d["conv_w"][l][j].reshape(88, 128).T
    cw[:, 3, :] = d["conv_b"][l].reshape(88, 128).T
    return vecs, cw


def _halo(a, c):
    out = np.zeros((1024 + HALO, a.shape[1]), np.float32)
    out[HALO:] = a[c * 1024:(c + 1) * 1024]
    if c > 0:
        out[:HALO] = a[c * 1024 - HALO:c * 1024]
    return np.ascontiguousarray(out.T)


def kernel(**inputs):
    d = {k: np.asarray(v) for k, v in inputs.items()}
    x = np.ascontiguousarray(d["x"][0]).astype(np.float32)
    cores = list(range(8))
    depth = d["w_in"].shape[0]
    for l in range(depth):
        n1w = np.ascontiguousarray(d["norm1_w"][l].reshape(16, 128).T)
        lnw = np.ascontiguousarray(d["gmlp_ln_w"][l].reshape(4, 128).T)
        w_in = np.ascontiguousarray(d["w_in"][l])
        in_maps = [{"xT": np.ascontiguousarray(x[c * 1024:(c + 1) * 1024].T), "n1w": n1w, "lnw": lnw, "w_in": w_in} for c in cores]
        res = run_bass_kernel_spmd(_prog("A"), in_maps, core_ids=cores)
        proj = np.concatenate([r["projT"].T for r in res.results], axis=0)
        del in_maps, res
        in_maps = [prep_B(proj, d, l, c) for c in cores]
        res = run_bass_kernel_spmd(_prog("B"), in_maps, core_ids=cores)
        mixed = np.empty((8192, 2048), np.float32)
        for c in range(6):
            mixed[:, c * 128:(c + 1) * 128] = res.results[c]["ret_o"]
            mixed[:, 768 + c * 128:768 + (c + 1) * 128] = res.results[c]["att_o"]
        for c in range(8):
            gg, hh = c // 2, c % 2
            mixed[hh * 4096:(hh + 1) * 4096, 1536 + gg * 128:1536 + (gg + 1) * 128] = res.results[c]["gm_o"]
        del in_maps, res, proj
        vecs, cw = _prep_C_common(d, l)
        w_out = np.ascontiguousarray(d["w_out"][l])
        w_up = np.ascontiguousarray(d["w_up"][l])
        w_down = np.ascontiguousarray(d["w_down"][l])
        in_maps = [{"xT": _halo(x, c), "mT": _halo(mixed, c), "vecs": vecs, "cw": cw, "w_out": w_out, "w_up": w_up, "w_down": w_down} for c in cores]
        res = run_bass_kernel_spmd(_prog("C" if l < depth - 1 else "CF"), in_maps, core_ids=cores)
        x = np.concatenate([r["outT"].T for r in res.results], axis=0).astype(np.float32)
        del in_maps, res, mixed
    return np.ascontiguousarray(x[None]).astype(np.float32)
```
